# Optimizing a Trainium2 kernel written in Bass

```python
import jax, jax.numpy as jnp
from jax import lax
import numpy as np

D_MODEL = 1024
BATCH = 4
SEQ = 4096
DEPTH = 2

GRID_W = 64
CTX_LEN = 256
NORM_EPS = 1e-6
NEG_INF = -1e30
MIX_W = D_MODEL // 2
N_BRANCH = 3
GLA_HEADS = 4
GLA_DV = MIX_W // GLA_HEADS
GLA_DK = GLA_DV // 2
GLA_GATE_RANK = 16
GLA_GATE_NORM = 16.0
GLA_CHUNK = 64
SWA_HEAD_DIM = 64
SWA_Q_HEADS = MIX_W // SWA_HEAD_DIM
SWA_KV_HEADS = SWA_Q_HEADS // 4
SWA_GROUP = SWA_Q_HEADS // SWA_KV_HEADS
SWA_WINDOW = 128
SWA_BLOCK = 128
ROPE_FREQS = SWA_HEAD_DIM // 4
ROPE_BASE = 10000.0
RWKV_HEAD = 64
RWKV_HEADS = MIX_W // RWKV_HEAD
RWKV_DECAY_RANK = 64
RWKV_A_RANK = 64
RWKV_GATE_RANK = 128
RWKV_GN_EPS = 64e-5
FFN_HIDDEN = -(-8 * D_MODEL // (3 * 256)) * 256
GLA_SPLIT = (GLA_HEADS * GLA_DK, GLA_HEADS * GLA_DK, MIX_W, GLA_GATE_RANK, GLA_GATE_RANK, MIX_W)
SWA_SPLIT = (SWA_Q_HEADS * SWA_HEAD_DIM, SWA_KV_HEADS * SWA_HEAD_DIM, SWA_KV_HEADS * SWA_HEAD_DIM)
RWKV_SPLIT = (MIX_W, MIX_W, MIX_W, RWKV_DECAY_RANK, RWKV_A_RANK, RWKV_GATE_RANK)
GROUP_SPLIT = (sum(GLA_SPLIT), sum(SWA_SPLIT), sum(RWKV_SPLIT), N_BRANCH * D_MODEL)
IN_WIDTH = sum(GROUP_SPLIT)

kernel_name = 'hybrid_gla_swa_rwkv7_dit_block'


def rmsnorm(x, g):
    x32 = x.astype(jnp.float32)
    y = x32 * lax.rsqrt(jnp.mean(x32 * x32, axis=-1, keepdims=True) + NORM_EPS)
    return (y * g.astype(jnp.float32)).astype(x.dtype)


def modulate(h, shift, scale):
    return h * (1 + scale[:, None]) + shift[:, None]


def swiglu(h, w1, w3, w2):
    return (jax.nn.silu(h @ w1) * (h @ w3)) @ w2


def split_cols(a, widths):
    return jnp.split(a, np.cumsum(widths)[:-1].tolist(), axis=-1)


def flip(a):
    return a[:, ::-1]


def centred_shift(f):
    z = jnp.zeros_like(f[:, :1])
    return 0.5 * (jnp.concatenate([z, f[:, :-1]], axis=1) + jnp.concatenate([f[:, 1:], z], axis=1))


def axial_rope(T):
    rows = T // GRID_W
    row = jnp.repeat(jnp.arange(rows), GRID_W).astype(jnp.float32)
    col = jnp.tile(jnp.arange(GRID_W), rows).astype(jnp.float32)
    inv = ROPE_BASE ** (-jnp.arange(ROPE_FREQS, dtype=jnp.float32) / ROPE_FREQS)
    ang = jnp.stack([row[:, None] * inv, col[:, None] * inv], axis=1)
    return jnp.cos(ang), jnp.sin(ang)


def apply_rope(x, cos, sin):
    B, T, H, Dh = x.shape
    xs = x.reshape(B, T, H, 2, 2, ROPE_FREQS)
    x1, x2 = xs[..., 0, :], xs[..., 1, :]
    cs, sn = cos[None, :, None].astype(x.dtype), sin[None, :, None].astype(x.dtype)
    return jnp.stack([x1 * cs - x2 * sn, x2 * cs + x1 * sn], axis=-2).reshape(B, T, H, Dh)


def gla_features(p, w_gk2, b_gk):
    q, k, v, gkf, gkb, og = split_cols(p, GLA_SPLIT)
    B, T, _ = q.shape
    heads_k = lambda t: t.reshape(B, T, GLA_HEADS, GLA_DK)
    log_gate = lambda lr, d: heads_k(jax.nn.log_sigmoid((lr @ w_gk2[d] + b_gk[d]).astype(jnp.float32)) / GLA_GATE_NORM)
    return (heads_k(q) * GLA_DK ** -0.5, heads_k(k), v.reshape(B, T, GLA_HEADS, GLA_DV),
            log_gate(gkf, 0), log_gate(gkb, 1), og)


def gla_scan(q, k, v, log_g, s0, with_out):
    B, T, H, K = k.shape
    V = v.shape[-1]
    n = T // GLA_CHUNK
    chunks = lambda a: a.astype(jnp.float32).reshape(B, n, GLA_CHUNK, H, a.shape[-1]).transpose(1, 0, 3, 2, 4)
    lower = jnp.tril(jnp.ones((GLA_CHUNK, GLA_CHUNK), bool))[:, :, None]
    xs = (chunks(k), chunks(v), chunks(log_g)) + ((chunks(q),) if with_out else ())

    def step(s, inp):
        kc, vc, gc = inp[:3]
        b = jnp.cumsum(gc, axis=2)
        b_last = b[:, :, -1:]
        s_new = jnp.exp(b_last[:, :, 0])[..., None] * s + jnp.einsum('bhck,bhcv->bhkv', kc * jnp.exp(b_last - b), vc)
        if not with_out:
            return s_new, None
        qc = inp[3]
        o_inter = jnp.einsum('bhck,bhkv->bhcv', qc * jnp.exp(b), s)
        rel = jnp.exp(jnp.where(lower, b[:, :, :, None] - b[:, :, None], -jnp.inf))
        att = jnp.einsum('bhik,bhjk,bhijk->bhij', qc, kc, rel)
        return s_new, o_inter + jnp.einsum('bhij,bhjv->bhiv', att, vc)

    s_fin, o = lax.scan(step, s0, xs)
    if not with_out:
        return s_fin, None
    return s_fin, o.transpose(1, 0, 3, 2, 4).reshape(B, T, H, V)


def gla_out(o, og, g_gla):
    B, T, H, V = o.shape
    return (rmsnorm(o, g_gla).reshape(B, T, H * V) * jax.nn.silu(og.astype(jnp.float32))).astype(og.dtype)


def gla_branch(fz, fx, g_gla, need_ctx):
    qz, kz, vz, lfz, lbz, ogz = fz
    qx, kx, vx, lfx, lbx, ogx = fx
    s0 = jnp.zeros((kx.shape[0], GLA_HEADS, GLA_DK, GLA_DV), jnp.float32)
    sf, ozf = gla_scan(qz, kz, vz, lfz, s0, need_ctx)
    sb, ozb = gla_scan(flip(qz), flip(kz), flip(vz), flip(lbz), s0, need_ctx)
    _, oxf = gla_scan(qx, kx, vx, lfx, sf, True)
    _, oxb = gla_scan(flip(qx), flip(kx), flip(vx), flip(lbx), sb, True)
    yx = gla_out(oxf + flip(oxb), ogx, g_gla)
    yz = gla_out(ozf + flip(ozb), ogz, g_gla) if need_ctx else None
    return yz, yx


def swa_features(p):
    q, k, v = split_cols(p, SWA_SPLIT)
    B, T, _ = q.shape
    return (q.reshape(B, T, SWA_Q_HEADS, SWA_HEAD_DIM), k.reshape(B, T, SWA_KV_HEADS, SWA_HEAD_DIM),
            v.reshape(B, T, SWA_KV_HEADS, SWA_HEAD_DIM))


def windowed_attention(q, k, v, k_ctx, v_ctx, sink):
    B, T, Hq, Dh = q.shape
    nb = T // SWA_BLOCK
    qb = q.reshape(B, nb, SWA_BLOCK, SWA_KV_HEADS, SWA_GROUP, Dh) * Dh ** -0.5

    def neighbours(a):
        ab = jnp.pad(a, ((0, 0), (SWA_BLOCK, SWA_BLOCK), (0, 0), (0, 0))).reshape(B, nb + 2, SWA_BLOCK, SWA_KV_HEADS, Dh)
        return jnp.concatenate([ab[:, :-2], ab[:, 1:-1], ab[:, 2:]], axis=2)

    kw, vw = neighbours(k), neighbours(v)
    s_loc = jnp.einsum('bnqhgd,bnkhd->bhgnqk', qb, kw).astype(jnp.float32)
    s_ctx = jnp.einsum('bnqhgd,bchd->bhgnqc', qb, k_ctx).astype(jnp.float32)
    blk = jnp.arange(nb)[:, None]
    qpos = (blk * SWA_BLOCK + jnp.arange(SWA_BLOCK)[None, :])[:, :, None]
    kpos = ((blk - 1) * SWA_BLOCK + jnp.arange(3 * SWA_BLOCK)[None, :])[:, None, :]
    band = (jnp.abs(kpos - qpos) <= SWA_WINDOW) & (kpos >= 0) & (kpos < T)
    sink_b = jnp.broadcast_to(sink.astype(jnp.float32).reshape(1, SWA_KV_HEADS, SWA_GROUP, 1, 1, 1), s_loc.shape[:-1] + (1,))
    p = jax.nn.softmax(jnp.concatenate([jnp.where(band, s_loc, NEG_INF), s_ctx, sink_b], axis=-1), axis=-1)
    p_loc = p[..., :3 * SWA_BLOCK].astype(v.dtype)
    p_ctx = p[..., 3 * SWA_BLOCK:-1].astype(v.dtype)
    out = jnp.einsum('bhgnqk,bnkhd->bnqhgd', p_loc, vw) + jnp.einsum('bhgnqc,bchd->bnqhgd', p_ctx, v_ctx)
    return out.reshape(B, T, Hq * Dh)


def context_attention(q, k, v, sink):
    B, C, Hq, Dh = q.shape
    qg = q.reshape(B, C, SWA_KV_HEADS, SWA_GROUP, Dh) * Dh ** -0.5
    s = jnp.einsum('bqhgd,bkhd->bhgqk', qg, k).astype(jnp.float32)
    sink_b = jnp.broadcast_to(sink.astype(jnp.float32).reshape(1, SWA_KV_HEADS, SWA_GROUP, 1, 1), s.shape[:-1] + (1,))
    p = jax.nn.softmax(jnp.concatenate([s, sink_b], axis=-1), axis=-1)[..., :-1].astype(v.dtype)
    return jnp.einsum('bhgqk,bkhd->bqhgd', p, v).reshape(B, C, Hq * Dh)


def rwkv_features(p, mu_shift, w0, w_w2, a0, w_a2, w_g2, k_k, k_a):
    p = p + mu_shift * (centred_shift(p) - p)
    r, k, v, wl, al, gl = split_cols(p, RWKV_SPLIT)
    B, T, _ = r.shape
    heads = lambda t: t.reshape(B, T, RWKV_HEADS, RWKV_HEAD)
    g = jax.nn.sigmoid(gl) @ w_g2
    kk = heads(k * k_k).astype(jnp.float32)
    kk = kk / jnp.maximum(jnp.sqrt(jnp.sum(kk * kk, axis=-1, keepdims=True)), 1e-12)
    dirs = []
    for d in range(2):
        w_raw = -jax.nn.softplus(-(w0[d] + jnp.tanh(wl) @ w_w2[d]).astype(jnp.float32)) - 0.5
        decay = jnp.exp(-jnp.exp(w_raw))
        a = jax.nn.sigmoid(a0[d] + al @ w_a2[d])
        dirs.append((heads(decay), heads(k * (1 + (a - 1) * k_a)), heads(a)))
    return heads(r), heads(v), kk, g, dirs


def rwkv7_scan(r, w, k, v, kk, a, s0, with_out):
    tm = lambda t: jnp.moveaxis(t.astype(jnp.float32), 1, 0)
    xs = tuple(tm(t) for t in (w, k, v, kk, a)) + ((tm(r),) if with_out else ())

    def step(S, inp):
        w_t, k_t, v_t, kk_t, a_t = inp[:5]
        sa = jnp.einsum('bhvk,bhk->bhv', S, kk_t)
        S = S * w_t[:, :, None, :] - sa[..., None] * (kk_t * a_t)[:, :, None, :] + v_t[..., None] * k_t[:, :, None, :]
        if not with_out:
            return S, None
        return S, jnp.einsum('bhvk,bhk->bhv', S, inp[5])

    s_fin, y = lax.scan(step, s0, xs)
    if not with_out:
        return s_fin, None
    return s_fin, jnp.moveaxis(y, 0, 1)


def rwkv_out(y, r, kf, kb, v, g, r_k, gn_w, gn_b):
    B, T, H, N = y.shape
    mean = jnp.mean(y, axis=-1, keepdims=True)
    var = jnp.mean(jnp.square(y - mean), axis=-1, keepdims=True)
    yn = ((y - mean) * lax.rsqrt(var + RWKV_GN_EPS)).reshape(B, T, H * N) * gn_w + gn_b
    bonus = jnp.sum(r.astype(jnp.float32) * (kf + kb).astype(jnp.float32) * r_k, axis=-1, keepdims=True) * v.astype(jnp.float32)
    return ((yn + bonus.reshape(B, T, H * N)) * g.astype(jnp.float32)).astype(g.dtype)


def rwkv_branch(fz, fx, r_k, gn_w, gn_b, need_ctx):
    def run(f, s_f, s_b, with_out):
        r, v, kk, g, ((wf, kf, af), (wb, kb, ab)) = f
        sf, yf = rwkv7_scan(r, wf, kf, v, kk, af, s_f, with_out)
        sb, yb = rwkv7_scan(flip(r), flip(wb), flip(kb), flip(v), flip(kk), flip(ab), s_b, with_out)
        if not with_out:
            return sf, sb, None
        return sf, sb, rwkv_out(yf + flip(yb), r, kf, kb, v, g, r_k, gn_w, gn_b)

    s0 = jnp.zeros((fx[0].shape[0], RWKV_HEADS, RWKV_HEAD, RWKV_HEAD), jnp.float32)
    szf, szb, yz = run(fz, s0, s0, need_ctx)
    _, _, yx = run(fx, szf, szb, True)
    return yz, yx


def merge_branches(ys, gate_logits, w_branch, w_out):
    gates = jnp.split(gate_logits, N_BRANCH, axis=-1)
    m = jax.nn.sigmoid(gates[0]) * (ys[0] @ w_branch[0])
    for i in range(1, N_BRANCH):
        m = m + jax.nn.sigmoid(gates[i]) * (ys[i] @ w_branch[i])
    return m @ w_out


def mixer_sublayer(hz, hx, cos, sin, w_in, w_gk2, b_gk, g_gla, sink, mu_shift, w0, w_w2, a0, w_a2,
                   w_g2, k_k, k_a, r_k, gn_w, gn_b, w_branch, w_out, need_ctx):
    gla_z, swa_z, rwkv_z, gate_z = split_cols(hz @ w_in, GROUP_SPLIT)
    gla_x, swa_x, rwkv_x, gate_x = split_cols(hx @ w_in, GROUP_SPLIT)
    ya_z, ya_x = gla_branch(gla_features(gla_z, w_gk2, b_gk), gla_features(gla_x, w_gk2, b_gk), g_gla, need_ctx)
    qz, kz, vz = swa_features(swa_z)
    qx, kx, vx = swa_features(swa_x)
    yb_x = windowed_attention(apply_rope(qx, cos, sin), apply_rope(kx, cos, sin), vx, kz, vz, sink)
    rw = (mu_shift, w0, w_w2, a0, w_a2, w_g2, k_k, k_a)
    yc_z, yc_x = rwkv_branch(rwkv_features(rwkv_z, *rw), rwkv_features(rwkv_x, *rw), r_k, gn_w, gn_b, need_ctx)
    mx = merge_branches((ya_x, yb_x, yc_x), gate_x, w_branch, w_out)
    if not need_ctx:
        return None, mx
    yb_z = context_attention(qz, kz, vz, sink)
    mz = merge_branches((ya_z, yb_z, yc_z), gate_z, w_branch, w_out)
    return mz, mx


def setup_inputs(seed: int = 0) -> dict:
    key = jax.random.key(seed)
    ks = iter(jax.random.split(key, 32))
    f32 = jnp.float32
    nrm = lambda shape, scale: jax.random.normal(next(ks), shape, f32) * scale
    L, D = DEPTH, D_MODEL
    return {
        'x': nrm((BATCH, SEQ, D), 1.0),
        'c': nrm((BATCH, D), 1.0),
        'ctx': nrm((BATCH, CTX_LEN, D), 1.0),
        'c_ctx': nrm((D,), 1.0),
        'w_ada': nrm((L, D, 6 * D), 0.5 * D ** -0.5),
        'b_ada': nrm((L, 6 * D), 0.02),
        'g_mix': 1.0 + nrm((L, D), 0.02),
        'g_ffn': 1.0 + nrm((L, D), 0.02),
        'w_in': nrm((L, D, IN_WIDTH), D ** -0.5),
        'w_gk2': nrm((L, 2, GLA_GATE_RANK, GLA_HEADS * GLA_DK), GLA_GATE_RANK ** -0.5),
        'b_gk': nrm((L, 2, GLA_HEADS * GLA_DK), 0.5),
        'g_gla': 1.0 + nrm((L, GLA_DV), 0.02),
        'sink': nrm((L, SWA_Q_HEADS), 0.5),
        'mu_shift': jax.random.uniform(next(ks), (L, sum(RWKV_SPLIT)), f32),
        'w0': jax.random.uniform(next(ks), (L, 2, MIX_W), f32, -6.0, 1.0),
        'w_w2': nrm((L, 2, RWKV_DECAY_RANK, MIX_W), 0.5 * RWKV_DECAY_RANK ** -0.5),
        'a0': nrm((L, 2, MIX_W), 0.5),
        'w_a2': nrm((L, 2, RWKV_A_RANK, MIX_W), 0.5 * RWKV_A_RANK ** -0.5),
        'w_g2': nrm((L, RWKV_GATE_RANK, MIX_W), RWKV_GATE_RANK ** -0.5),
        'k_k': 0.85 + nrm((L, MIX_W), 0.05),
        'k_a': 1.0 + nrm((L, MIX_W), 0.05),
        'r_k': nrm((L, RWKV_HEADS, RWKV_HEAD), 0.1),
        'gn_w': 1.0 + nrm((L, MIX_W), 0.02),
        'gn_b': nrm((L, MIX_W), 0.02),
        'w_branch': nrm((L, N_BRANCH, MIX_W, D), MIX_W ** -0.5),
        'w_out': nrm((L, D, D), D ** -0.5),
        'w_ffn1': nrm((L, D, FFN_HIDDEN), D ** -0.5),
        'w_ffn3': nrm((L, D, FFN_HIDDEN), D ** -0.5),
        'w_ffn2': nrm((L, FFN_HIDDEN, D), FFN_HIDDEN ** -0.5),
        'g_final': 1.0 + nrm((D,), 0.02),
    }


def reference(x, c, ctx, c_ctx, w_ada, b_ada, g_mix, g_ffn, w_in, w_gk2, b_gk, g_gla, sink, mu_shift,
              w0, w_w2, a0, w_a2, w_g2, k_k, k_a, r_k, gn_w, gn_b, w_branch, w_out, w_ffn1, w_ffn3,
              w_ffn2, g_final):
    cos, sin = axial_rope(x.shape[1])
    z = ctx
    for l in range(DEPTH):
        need_ctx = l < DEPTH - 1
        mod_x = jax.nn.silu(c) @ w_ada[l] + b_ada[l]
        mod_z = (jax.nn.silu(c_ctx) @ w_ada[l] + b_ada[l])[None]
        sh1x, sc1x, ga1x, sh2x, sc2x, ga2x = jnp.split(mod_x, 6, axis=-1)
        sh1z, sc1z, ga1z, sh2z, sc2z, ga2z = jnp.split(mod_z, 6, axis=-1)
        hx = modulate(rmsnorm(x, g_mix[l]), sh1x, sc1x)
        hz = modulate(rmsnorm(z, g_mix[l]), sh1z, sc1z)
        mz, mx = mixer_sublayer(hz, hx, cos, sin, w_in[l], w_gk2[l], b_gk[l], g_gla[l], sink[l], mu_shift[l],
                                w0[l], w_w2[l], a0[l], w_a2[l], w_g2[l], k_k[l], k_a[l], r_k[l], gn_w[l],
                                gn_b[l], w_branch[l], w_out[l], need_ctx)
        x = x + ga1x[:, None] * mx
        x = x + ga2x[:, None] * swiglu(modulate(rmsnorm(x, g_ffn[l]), sh2x, sc2x), w_ffn1[l], w_ffn3[l], w_ffn2[l])
        if need_ctx:
            z = z + ga1z[:, None] * mz
            z = z + ga2z[:, None] * swiglu(modulate(rmsnorm(z, g_ffn[l]), sh2z, sc2z), w_ffn1[l], w_ffn3[l], w_ffn2[l])
    return rmsnorm(x, g_final)
```

```python
import contextlib
import numpy as np
import concourse.bass as bass
import concourse.mybir as mybir
from concourse.bass_utils import run_bass_kernel_spmd

F32 = mybir.dt.float32
BF16 = mybir.dt.bfloat16
F32R = mybir.dt.float32r
AF = mybir.ActivationFunctionType
ALU = mybir.AluOpType
AX = mybir.AxisListType

D = 1024
INW = 7200
FH = 2816
ENGS = ("pe", "act", "dve", "pool", "sp")
DMA_Q = {"sp": 6, "pool": 4, "act": 2}


class Sched:
    def __init__(self, nc):
        self.nc = nc
        self.q = {e: [] for e in ENGS}
        self.cnt = {e: 0 for e in ENGS}
        self.seen = {e: {} for e in ENGS}
        self.buf = {}
        self.slot = {qe: 0 for qe in DMA_Q}
        self.vengs = []
        for qe, n in DMA_Q.items():
            for i in range(n):
                v = "%s_d%d" % (qe, i)
                self.vengs.append(v)
                self.cnt[v] = 0
        self.sems = {}
        self.ninstr = 0
        self.snaps = {}

    def _deps(self, reads, writes):
        deps = {}

        def need(f, n):
            if n > deps.get(f, 0):
                deps[f] = n

        for k in reads:
            b = self.buf.get(k)
            if b and b["w"]:
                need(*b["w"])
        for k in writes:
            b = self.buf.get(k)
            if b:
                if b["w"]:
                    need(*b["w"])
                for f, n in b["r"].items():
                    need(f, n)
        return deps

    def _waits(self, eng, deps):
        waits = []
        know = self.seen[eng]
        for f, n in sorted(deps.items(), key=lambda kv: -kv[1]):
            if n <= 0:
                continue
            if f == eng:
                if eng == "pe":
                    continue
                if eng in ("dve", "act") and n < self.cnt[eng]:
                    continue
            if know.get(f, 0) >= n:
                continue
            know[f] = n
            snap = self.snaps.get(f)
            if snap is not None and n - 1 < len(snap):
                for k2, v2 in snap[n - 1]:
                    if v2 > know.get(k2, 0):
                        know[k2] = v2
            waits.append((f, n))
        return waits

    def _snap(self, who, issuer):
        self.snaps.setdefault(who, []).append(tuple(self.seen[issuer].items()))

    def _record(self, who, n, reads, writes):
        for k in reads:
            b = self.buf.setdefault(k, {"w": None, "r": {}})
            b["r"][who] = n
        for k in writes:
            self.buf[k] = {"w": (who, n), "r": {}}

    def add(self, eng, fn, reads=(), writes=()):
        deps = self._deps(reads, writes)
        waits = self._waits(eng, deps)
        self.cnt[eng] += 1
        n = self.cnt[eng]
        self._snap(eng, eng)
        self.q[eng].append((waits, fn, eng))
        self._record(eng, n, reads, writes)
        self.ninstr += 1

    def dma(self, qe, out, in_, reads=(), writes=(), **kw):
        s = self.slot[qe]
        self.slot[qe] = (s + 1) % DMA_Q[qe]
        v = "%s_d%d" % (qe, s)
        deps = self._deps(reads, writes)
        if self.cnt[v] > deps.get(v, 0):
            deps[v] = self.cnt[v]
        waits = self._waits(qe, deps)
        self.cnt[v] += 1
        n = self.cnt[v]
        self._snap(v, qe)

        def fn(e, out=out, in_=in_, kw=kw):
            return e.dma_start(out=out, in_=in_, **kw)

        self.q[qe].append((waits, fn, v))
        self._record(v, n, reads, writes)
        self.ninstr += 1

    def barrier(self):
        allc = dict(self.cnt)
        for e in ENGS:
            w = self._waits(e, {f: n for f, n in allc.items() if f != e})
            if w:
                self.q[e].append((w, None, None))

    def emit(self):
        nc = self.nc
        with contextlib.ExitStack() as st:
            for e in list(ENGS) + self.vengs:
                self.sems[e] = st.enter_context(nc.semaphore("s_" + e))
            block = st.enter_context(nc.Block())

            def val(f, n):
                return n * 16 if f in self.vengs else n

            def run(engname, e):
                for waits, fn, who in self.q[engname]:
                    if fn is None:
                        for f, n in waits:
                            e.wait_ge(self.sems[f], val(f, n))
                        continue
                    for f, n in waits[:-1]:
                        e.wait_ge(self.sems[f], val(f, n))
                    ins = fn(e)
                    if waits:
                        f, n = waits[-1]
                        ins._wait_ge(self.sems[f], val(f, n))
                    ins.then_inc(self.sems[who], 16 if who in self.vengs else 1)
                if engname == "sp":
                    for v in self.vengs:
                        if self.cnt[v] > 0:
                            e.wait_ge(self.sems[v], 16 * self.cnt[v])
                    for f in ENGS:
                        if f != "sp" and self.cnt[f] > 0:
                            e.wait_ge(self.sems[f], self.cnt[f])

            @block.tensor
            def _(e):
                run("pe", e)

            @block.scalar
            def _(e):
                run("act", e)

            @block.vector
            def _(e):
                run("dve", e)

            @block.gpsimd
            def _(e):
                run("pool", e)

            @block.sync
            def _(e):
                run("sp", e)


class Ring:
    def __init__(self, mk, name, n, shape, dt=F32, psum=False):
        self.t = []
        for i in range(n):
            nm = "%s%d_%d" % (name, i, mk.uid)
            if psum:
                self.t.append(mk.st.enter_context(mk.nc.psum_tensor(nm, shape, dt)))
            else:
                self.t.append(mk.st.enter_context(mk.nc.sbuf_tensor(nm, shape, dt)))
        self.i = 0

    def get(self):
        t = self.t[self.i]
        self.i = (self.i + 1) % len(self.t)
        return t


class MK:
    def __init__(self, nc, st):
        self.nc = nc
        self.st = st
        self.S = Sched(nc)
        self.uid = 0
        self.stack = []

    def push(self):
        self.stack.append(self.st)
        self.st = contextlib.ExitStack()
        self.st.__enter__()
        self.uid += 1

    def pop(self):
        self.S.barrier()
        self.st.close()
        self.st = self.stack.pop()

    def sb(self, name, shape, dt=F32):
        return self.st.enter_context(self.nc.sbuf_tensor("%s_%d" % (name, self.uid), shape, dt))

    def _k(self, aps, km):
        out = []
        for a in aps:
            if a is None or isinstance(a, (int, float)):
                continue
            if isinstance(a, (tuple, str)):
                out.append(a)
                continue
            n = a.name
            if km and n in km:
                out.extend(km[n])
            else:
                out.append(n)
        return out

    def op(self, eng, fn, outs, ins, km=None):
        rk = self._k(ins, km)
        wk = self._k(outs, km)
        ps = [k for k in rk if isinstance(k, str) and k.startswith("ps")]
        if ps:
            rk = [k for k in rk if k not in ps]
            wk = wk + [k for k in ps if k not in wk]
        self.S.add(eng, fn, reads=rk, writes=wk)

    def mm(self, out, lhsT, rhs, start=True, stop=True, km=None):
        self.op("pe", lambda e: e.matmul(out, lhsT, rhs, start=start, stop=stop), [out], [lhsT, rhs], km)

    def tr(self, out, in_, ident, km=None):
        self.op("pe", lambda e: e.transpose(out, in_, ident), [out], [in_, ident], km)

    def act(self, out, in_, func, bias=0.0, scale=1.0, accum=None, km=None):
        kw = {}
        if accum is not None:
            kw["accum_out"] = accum
        self.op("act", lambda e: e.activation(out=out, in_=in_, func=func, bias=bias, scale=scale, **kw),
                [out, accum], [in_, bias, scale], km)

    def tt(self, eng, out, a, b, op, km=None):
        self.op(eng, lambda e: e.tensor_tensor(out=out, in0=a, in1=b, op=op), [out], [a, b], km)

    def ts(self, eng, out, a, s1, s2=None, op0=ALU.mult, op1=ALU.bypass, accum=None, km=None):
        kw = {}
        if accum is not None:
            kw["accum_out"] = accum
        self.op(eng, lambda e: e.tensor_scalar(out, a, s1, s2, op0, op1, **kw), [out, accum], [a, s1, s2], km)

    def stt(self, eng, out, in0, scalar, in1, op0, op1, km=None):
        self.op(eng, lambda e: e.scalar_tensor_tensor(out=out, in0=in0, scalar=scalar, in1=in1, op0=op0, op1=op1),
                [out], [in0, scalar, in1], km)

    def cp(self, eng, out, in_, km=None):
        if eng == "act":
            self.op(eng, lambda e: e.copy(out, in_), [out], [in_], km)
        else:
            self.op(eng, lambda e: e.tensor_copy(out, in_), [out], [in_], km)

    def memset(self, eng, out, v, km=None):
        self.op(eng, lambda e: e.memset(out, v), [out], [], km)

    def dma(self, q, out, in_, km=None, **kw):
        self.S.dma(q, out, in_, reads=self._k([in_], km), writes=self._k([out], km), **kw)

    def asel(self, out, in_, pattern, base, cm, cmp, fill):
        self.op("pool", lambda e: e.affine_select(out=out, in_=in_, pattern=pattern, base=base,
                                                  channel_multiplier=cm, compare_op=cmp, fill=fill), [out], [in_])


C_GQ, C_GK, C_GV, C_GKF, C_GKB, C_OG = 0, 256, 512, 1024, 1040, 1056
C_SQ, C_SK, C_SV = 1568, 2080, 2208
C_RW = 2336
C_RR, C_RK, C_RV, C_WL, C_AL, C_GL = 2336, 2848, 3360, 3872, 3936, 4000
C_GATE = 4128

W_NAMES = ["w_ada", "b_adaT", "g_mixT", "g_ffnT", "w_in", "w_gk2", "b_gk", "g_gla", "sink", "mu_shift", "w0",
           "w_w2", "a0", "w_a2", "w_g2", "k_k", "k_a", "r_k", "gn_w", "gn_b", "w_branch", "w_out", "w_ffn1",
           "w_ffn3", "w_ffn2", "g_final"]


def build(NZ=2, NX=32, L=2, GRID_W=64, dbg=None):
    NODUMP = False
    if dbg is not None and dbg.startswith("T:"):
        dbg = dbg[2:]
        NODUMP = True
    NT = NZ + NX
    TA = NT * 128
    nc = bass.Bass("TRN2", target_bir_lowering=False)

    def din(name, shape):
        return nc.dram_tensor(name, list(shape), F32, kind="ExternalInput").ap()

    xz = din("xz", [TA, D])
    cT = din("cT", [128, 8, 2])
    w_ada = din("w_ada", [L, D, 6 * D])
    b_adaT = din("b_adaT", [L, 128, 48])
    g_mixT = din("g_mixT", [L, 128, 8])
    g_ffnT = din("g_ffnT", [L, 128, 8])
    w_in = din("w_in", [L, D, INW])
    w_gk2 = din("w_gk2", [L, 2, 16, 256])
    b_gk = din("b_gk", [L, 2, 256])
    g_gla = din("g_gla", [L, 128])
    sink = din("sink", [L, 8])
    mu_shift = din("mu_shift", [L, 1792])
    w0 = din("w0", [L, 2, 512])
    w_w2 = din("w_w2", [L, 2, 64, 512])
    a0 = din("a0", [L, 2, 512])
    w_a2 = din("w_a2", [L, 2, 64, 512])
    w_g2 = din("w_g2", [L, 128, 512])
    k_k = din("k_k", [L, 512])
    k_a = din("k_a", [L, 512])
    r_k = din("r_k", [L, 512])
    gn_w = din("gn_w", [L, 512])
    gn_b = din("gn_b", [L, 512])
    w_branch = din("w_branch", [L, 3, 512, D])
    w_out = din("w_out", [L, D, D])
    w_ffn1 = din("w_ffn1", [L, D, FH])
    w_ffn3 = din("w_ffn3", [L, D, FH])
    w_ffn2 = din("w_ffn2", [L, FH, D])
    g_final = din("g_final", [1, D])
    ropeC = din("ropeC", [NX * 128, 64])
    ropeS = din("ropeS", [NX * 128, 64])
    yout = nc.dram_tensor("yout", [NX * 128, D], F32, kind="ExternalOutput").ap()

    def dscr(name, shape):
        return nc.dram_tensor(name, list(shape), F32, kind="Internal").ap()

    P = dscr("P_scr", [TA, INW])
    XS = dscr("X_scr", [TA, D])
    MODROW = dscr("modrow", [2, 48 * 128])
    YA = dscr("YA_scr", [TA, 512])
    YB = dscr("YB_scr", [TA, 512])
    YC = dscr("YC_scr", [TA, 512])
    OF = dscr("OF_scr", [TA, 512])
    YF = dscr("YF_scr", [TA, 512])
    KF = dscr("KF_scr", [TA, 512])
    RG = dscr("RG_scr", [TA, 512])
    W13B = nc.dram_tensor("W13B_scr", [22, 128, 2048], BF16, kind="Internal").ap()
    dbg_out = None
    if dbg is not None:
        dbg_out = nc.dram_tensor("dbg", [TA, INW], F32, kind="ExternalOutput").ap()

    with contextlib.ExitStack() as st:
        mk = MK(nc, st)
        S = mk.S
        ident = mk.sb("ident", [128, 128])
        mk.memset("pool", ident[:], 1.0)
        mk.asel(ident[:], ident[:], [[-1, 128]], 0, 1, ALU.is_equal, 0.0)
        def mask(name, pat, cm, cmp):
            t = mk.sb(name, [128, 128])
            mk.memset("pool", t[:], 1.0)
            mk.asel(t[:], t[:], [[pat, 128]], 0, cm, cmp, 0.0)
            return t
        M_GE = mask("M_GE", -1, 1, ALU.is_ge)
        M_GT = mask("M_GT", -1, 1, ALU.is_gt)
        M_LE = mask("M_LE", 1, -1, ALU.is_ge)
        M_LT = mask("M_LT", 1, -1, ALU.is_gt)
        SBLK = mk.sb("SBLK", [128, 128])
        mk.memset("pool", SBLK[:], 0.0)
        mk.memset("pool", SBLK[0:64, 0:64], 1.0)
        mk.memset("pool", SBLK[64:128, 64:128], 1.0)
        BM = {}
        for nm, m_ in (("GE", M_GE), ("GT", M_GT), ("LE", M_LE), ("LT", M_LT)):
            BM[nm] = mk.sb("B" + nm, [128, 128])
            mk.tt("dve", BM[nm][:], m_[:], SBLK[:], ALU.mult)
        ones = mk.sb("ones", [128, 2])
        mk.memset("pool", ones[:], 1.0)
        cind = mk.sb("cind", [128, 2])
        mk.memset("pool", cind[:], 0.0)
        mk.memset("pool", cind[0:64, 0:1], 1.0)
        mk.memset("pool", cind[64:128, 1:2], 1.0)
        identb = mk.sb("identb", [128, 128], BF16)
        mk.cp("dve", identb[:], ident[:])
        psA = Ring(mk, "psA", 6, [128, 512], F32, psum=True)
        psB = Ring(mk, "psB", 2, [128, 1024], BF16, psum=True)
        scT = mk.sb("scT", [128, 8, 2])
        mk.dma("sp", scT[:], cT[:, :, :])
        mk.act(scT[:], scT[:], AF.Silu)
        modsb = mk.sb("modsb", [128, 48, 2])
        badat = mk.sb("badat", [128, 48])
        gmix = mk.sb("gmix", [128, 8])
        gffn = mk.sb("gffn", [128, 8])
        A1 = mk.sb("A1", [128, 8, 2])
        A2 = mk.sb("A2", [128, 8, 2])

        def hkey(ti):
            return {hT[:].name: [("hT", ti)]}

        def norm_to_hT(src_ap_fn, Acoef, Bcoef_j0, skip_z):
            for ti in range(NT):
                if skip_z and ti < NZ:
                    continue
                w = 1 if ti < NZ else 0
                xt = xin.get()
                mk.dma("sp", xt[:], src_ap_fn(ti), km={src_ap_fn(ti).name: [(src_ap_fn(ti).name, ti)]})
                ss = st1.get()
                mk.act(sq[:], xt[:], AF.Square, accum=ss[:])
                mk.act(ss[:], ss[:], AF.Ln, bias=1e-6, scale=1.0 / D)
                mk.act(ss[:], ss[:], AF.Exp, scale=-0.5)
                dgt = dg.get()
                mk.ts("dve", dgt[:], ident[:], ss[:, 0:1])
                for half in range(2):
                    ps = psA.get()
                    for kk in range(4):
                        k = half * 4 + kk
                        mk.mm(ps[:, kk * 128:(kk + 1) * 128], xt[:, k * 128:(k + 1) * 128], dgt[:])
                    for kk in range(4):
                        k = half * 4 + kk
                        mk.act(hT[:, k, ti * 128:(ti + 1) * 128], ps[:, kk * 128:(kk + 1) * 128], AF.Identity,
                               bias=modsb[:, Bcoef_j0 + k, w:w + 1], scale=Acoef[:, k, w:w + 1], km=hkey(ti))

        for l in range(L):
            need_ctx = l < L - 1
            mk.push()
            hT = mk.sb("hT", [128, 8, TA], BF16)
            wst = Ring(mk, "wst", 2, [128, 8, 512], F32)
            wbf = Ring(mk, "wbf", 2, [128, 8, 512], BF16)
            xin = Ring(mk, "xin", 2, [128, D], F32)
            sq = mk.sb("sqjunk", [128, D])
            st1 = Ring(mk, "st1", 4, [128, 1], F32)
            dg = Ring(mk, "dg", 2, [128, 128], F32)
            ev = Ring(mk, "ev", 6, [128, 512], F32)
            hs = Ring(mk, "hs", 2, [128, 8, 128], BF16)
            mu_om = mk.sb("mu_om", [128, 1792])
            mu_h = mk.sb("mu_h", [128, 1792])
            mk.dma("sp", badat[:], b_adaT[l])
            mk.dma("sp", gmix[:], g_mixT[l])
            mk.dma("sp", gffn[:], g_ffnT[l])
            wv = w_ada[l].rearrange("(k p) n -> p k n", p=128)
            for g in range(12):
                wt = wst.get()
                mk.dma("sp", wt[:], wv[:, :, g * 512:(g + 1) * 512])
                ps = psA.get()
                for j in range(4):
                    for k in range(8):
                        mk.mm(ps[:, 2 * j:2 * j + 2], wt[:, k, j * 128:(j + 1) * 128], scT[:, k, :],
                              start=(k == 0), stop=(k == 7))
                for w in range(2):
                    mk.tt("dve", modsb[:, g * 4:(g + 1) * 4, w], ps[:, 0:8].rearrange("p (j w) -> p j w", w=2)[:, :, w],
                          badat[:, g * 4:(g + 1) * 4], ALU.add)
            for w in range(2):
                mk.stt("dve", A1[:, :, w], modsb[:, 8:16, w], 1.0, gmix[:], ALU.add, ALU.mult)
                mk.stt("dve", A2[:, :, w], modsb[:, 32:40, w], 1.0, gffn[:], ALU.add, ALU.mult)
                mk.dma("pool", MODROW[w].rearrange("(j p) -> p j", p=128), modsb[:, :, w],
                       allow_slow_non_contiguous=True)
            src = (lambda ti: xz[ti * 128:(ti + 1) * 128, :]) if l == 0 else (lambda ti: XS[ti * 128:(ti + 1) * 128, :])
            norm_to_hT(src, A1, 0, False)
            mk.dma("sp", mu_om[:], mu_shift[l:l + 1, :].broadcast_to([128, 1792]))
            mk.ts("dve", mu_h[:], mu_om[:], 0.5)
            mk.ts("dve", mu_om[:], mu_om[:], -1.0, 1.0, ALU.mult, ALU.add)
            groups = []
            for (a, b, rw) in ((0, C_RW, False), (C_RW, C_GATE, True), (C_GATE, INW, False)):
                c = a
                while c < b:
                    groups.append((c, min(c + 512, b), rw))
                    c += 512
            wiv = w_in[l].rearrange("(k p) n -> p k n", p=128)
            for (c0, c1, rw) in groups:
                wd = c1 - c0
                wt = wst.get()
                mk.dma("sp", wt[:, :, 0:wd], wiv[:, :, c0:c1])
                wb = wbf.get()
                mk.cp("pool", wb[:, :, 0:wd], wt[:, :, 0:wd])
                for ti in range(NT):
                    if c0 >= C_GATE and ti < NZ and not need_ctx:
                        continue
                    ps = psA.get()
                    for k in range(8):
                        mk.mm(ps[:, 0:wd], hT[:, k, ti * 128:(ti + 1) * 128], wb[:, k, 0:wd], start=(k == 0),
                              stop=(k == 7), km=hkey(ti))
                    e1 = ev.get()
                    if not rw:
                        mk.cp("act", e1[:, 0:wd], ps[:, 0:wd])
                    else:
                        t0 = ti * 128
                        first = ti in (0, NZ)
                        last = ti in (NZ - 1, NT - 1)
                        h2 = hs.get()
                        lo = 1 if first else 0
                        hi = 127 if last else 128
                        km3 = {hT[:].name: [("hT", x) for x in (ti - 1, ti, ti + 1) if 0 <= x < NT]}
                        mk.tt("dve", h2[:, :, lo:hi], hT[:, :, t0 + lo - 1:t0 + hi - 1], hT[:, :, t0 + lo + 1:t0 + hi + 1],
                              ALU.add, km=km3)
                        if first:
                            mk.cp("dve", h2[:, :, 0:1], hT[:, :, t0 + 1:t0 + 2], km=km3)
                        if last:
                            mk.cp("dve", h2[:, :, 127:128], hT[:, :, t0 + 126:t0 + 127], km=km3)
                        ps2 = psA.get()
                        for k in range(8):
                            mk.mm(ps2[:, 0:wd], h2[:, k, :], wb[:, k, 0:wd], start=(k == 0), stop=(k == 7))
                        m0 = c0 - C_RW
                        e2 = ev.get()
                        mk.tt("dve", e1[:, 0:wd], ps[:, 0:wd], mu_om[:, m0:m0 + wd], ALU.mult)
                        mk.tt("dve", e2[:, 0:wd], ps2[:, 0:wd], mu_h[:, m0:m0 + wd], ALU.mult)
                        mk.tt("pool", e1[:, 0:wd], e1[:, 0:wd], e2[:, 0:wd], ALU.add)
                    mk.dma("pool", P[ti * 128:(ti + 1) * 128, c0:c1], e1[:, 0:wd], km={"P_scr": [("P", ti)]})
            mk.pop()
            if dbg == "P%d" % l:
                break
            mk.push()
            ev = Ring(mk, "ev", 6, [128, 512], F32)
            if True:
                W2a = [mk.sb("W2a%d" % d_, [33, 256]) for d_ in range(2)]
                gglab = mk.sb("gglab", [128, 128])
                Sst = [[mk.sb("Sst%d%d" % (d_, p_), [128, 256]) for p_ in range(2)] for d_ in range(2)]
                gin = Ring(mk, "gin", 2, [128, 1568], F32)
                gkTa = Ring(mk, "gkTa", 2, [33, 128], F32)
                for t_ in gkTa.t:
                    mk.memset("pool", t_[:], 1.0)
                g256 = Ring(mk, "g256", 8, [128, 256], F32)
                qkT = Ring(mk, "qkT", 4, [128, 256], F32R)
                khR = Ring(mk, "khR", 2, [128, 256], F32R)
                gvr = Ring(mk, "gvr", 2, [128, 512], F32R)
                Sst_r = [[mk.sb("Sstr%d%d" % (d_, p_), [128, 256], F32R) for p_ in range(2)] for d_ in range(2)]
                attS = Ring(mk, "attS", 3, [128, 128], F32R)
                decs = Ring(mk, "decs", 2, [128, 2], F32)
                st4 = Ring(mk, "st4", 4, [128, 4], F32)
            for d_ in range(2):
                mk.memset("pool", W2a[d_][:], 0.0)
                mk.dma("sp", W2a[d_][16 * d_:16 * d_ + 16, :], w_gk2[l, d_])
                mk.dma("sp", W2a[d_][32:33, :], b_gk[l, d_:d_ + 1, :])
                for p_ in range(2):
                    mk.memset("pool", Sst[d_][p_][:], 0.0)
                    mk.cp("pool", Sst_r[d_][p_][:], Sst[d_][p_][:])
            mk.dma("sp", gglab[:], g_gla[l:l + 1, :].broadcast_to([128, 128]))
            evs = Ring(mk, "ev_s", 3, [128, 512], F32)

            class RV:
                def __init__(self, ts):
                    self.t, self.i = ts, 0

                def get(self):
                    t_ = self.t[self.i]
                    self.i = (self.i + 1) % len(self.t)
                    return t_
            psGt, psGo = RV([psA.t[0], psA.t[1]]), psA.t[2]
            psSt, psSo = RV([psA.t[3], psA.t[4]]), psA.t[5]
            if True:
                KTd = [mk.sb("KTd%d" % i_, [128, TA], BF16) for i_ in range(2)]
                VX = mk.sb("VX", [128, NT, 128], BF16)
                sin_ = Ring(mk, "sin_", 2, [128, 768], F32)
                rope_t = Ring(mk, "rope_t", 2, [128, 128], F32)
                rq = Ring(mk, "rq", 4, [128, 640], F32)
                kd = Ring(mk, "kd", 2, [128, 256], F32)
                qTr = Ring(mk, "qTr", 3, [128, 4, 128], BF16)
                sinkb = mk.sb("sinkb", [128, 8])
                nsinkb = mk.sb("nsinkb", [128, 8])
                maskL = mk.sb("maskL", [128, 384])
                mk.memset("pool", maskL[:], 0.0)
                mk.asel(maskL[:, 0:128], maskL[:, 0:128], [[1, 128]], 0, -1, ALU.is_ge, -1e30)
                mk.asel(maskL[:, 256:384], maskL[:, 256:384], [[-1, 128]], 0, 1, ALU.is_ge, -1e30)
                scS = Ring(mk, "scS", 2, [128, 640], F32)
                pbS = Ring(mk, "pbS", 2, [128, 640], BF16)
                pTS = Ring(mk, "pTS", 2, [128, 640], BF16)
                st1b = Ring(mk, "st1b", 12, [128, 1], F32)
            mk.dma("sp", sinkb[:], sink[l:l + 1, :].broadcast_to([128, 8]))
            mk.ts("dve", nsinkb[:], sinkb[:], -1.0)

            def swa_queries(ti, qT, loc):
                yb = evs.get()
                psO = psSo
                nctx = NZ
                for h in range(8):
                    kv, c_, b0 = h // 4, h // 2, (h % 2) * 64
                    sc = scS.get()
                    wl = 0
                    if loc is not None:
                        kt0, nk, mo = loc
                        wl = nk * 128
                        ps1 = psSt.get()
                        mk.mm(ps1[:, 0:wl], qT[b0:b0 + 64, c_, :], KTd[kv][b0:b0 + 64, kt0 * 128:kt0 * 128 + wl])
                        mk.tt("dve", sc[:, 0:wl], ps1[:, 0:wl], maskL[:, mo:mo + wl], ALU.add)
                    ps2 = psSt.get()
                    mk.mm(ps2[:, 0:nctx * 128], qT[b0:b0 + 64, c_, :], KTd[kv][b0:b0 + 64, 0:nctx * 128])
                    wt = wl + nctx * 128
                    mk.cp("act", sc[:, wl:wt], ps2[:, 0:nctx * 128])
                    yield
                    mx = st1b.get()
                    mk.op("dve", lambda e, mx=mx, sc=sc, wt=wt: e.reduce_max(out=mx[:], in_=sc[:, 0:wt], axis=AX.X), [mx[:]], [sc[:]])
                    mk.ts("dve", mx[:], mx[:], -0.125, nsinkb[:, h:h + 1], ALU.mult, ALU.min)
                    yield
                    pb = pbS.get()
                    sm = st1b.get()
                    mk.act(pb[:, 0:wt], sc[:, 0:wt], AF.Exp, bias=mx[:, 0:1], scale=0.125, accum=sm[:])
                    es = st1b.get()
                    mk.act(es[:], sinkb[:, h:h + 1], AF.Exp, bias=mx[:, 0:1])
                    mk.tt("dve", sm[:], sm[:], es[:], ALU.add)
                    mk.op("dve", lambda e, sm=sm: e.reciprocal(out=sm[:], in_=sm[:]), [sm[:]], [sm[:]])
                    yield
                    nch = wt // 128
                    psT = psB.get()
                    for c2 in range(nch):
                        mk.tr(psT[:, c2 * 128:(c2 + 1) * 128], pb[:, c2 * 128:(c2 + 1) * 128], identb[:])
                    pT = pTS.get()
                    mk.cp("act", pT[:, 0:wt], psT[:, 0:wt])
                    yield
                    for c2 in range(nch):
                        if loc is not None and c2 < loc[1]:
                            kt = loc[0] + c2
                        else:
                            kt = c2 - (loc[1] if loc is not None else 0)
                        mk.mm(psO[:, h * 64:(h + 1) * 64], pT[:, c2 * 128:(c2 + 1) * 128], VX[:, kt, kv * 64:(kv + 1) * 64],
                              start=(c2 == 0), stop=(c2 == nch - 1))
                    mk.ts("dve", yb[:, h * 64:(h + 1) * 64], psO[:, h * 64:(h + 1) * 64], sm[:, 0:1])
                    yield
                mk.dma("pool", YB[ti * 128:(ti + 1) * 128, :], yb[:], km={"YB_scr": [("YB_scr", ti)]})

            def gla_gen():
                for d_ in range(2):
                    order = list(range(NT)) if d_ == 0 else (list(range(NZ - 1, -1, -1)) + list(range(NT - 1, NZ - 1, -1)))
                    Mc, Mr = (M_LE, M_GT) if d_ == 0 else (M_GE, M_LT)
                    for ti in order:
                        want_o = need_ctx or ti >= NZ
                        r0_, r1_ = ti * 128, (ti + 1) * 128
                        gi = gin.get()
                        mk.dma("sp", gi[:], P[r0_:r1_, 0:1568], km={"P_scr": [("P", ti)]})
                        ps = psGt.get()
                        mk.tr(ps[0:32, 0:128], gi[:, C_GKF:C_GKF + 32], ident[:])
                        gk = gkTa.get()
                        mk.cp("act", gk[0:32, :], ps[0:32, 0:128])
                        yield
                        ps = psGt.get()
                        mk.mm(ps[:, 0:256], gk[0:33, :], W2a[d_][0:33, :])
                        lg = g256.get()
                        mk.act(lg[:], ps[:, 0:256], AF.Exp, scale=-1.0)
                        mk.act(lg[:], lg[:], AF.Ln, bias=1.0)
                        mk.ts("dve", lg[:], lg[:], -1.0 / 16.0)
                        yield
                        psC = psGt.get()
                        mk.mm(psC[:, 0:256], Mc[:], lg[:])
                        mk.mm(psC[:, 256:512], Mr[:], lg[:])
                        psD = psGt.get()
                        for p_ in range(2):
                            mk.mm(psD[:, p_:p_ + 1], lg[:, p_ * 128:(p_ + 1) * 128], ones[:, 0:1])
                        dc = decs.get()
                        mk.act(dc[:], psD[:, 0:2], AF.Exp)
                        yield
                        kh0 = g256.get()
                        mk.act(kh0[:], psC[:, 256:512], AF.Exp)
                        khat = khR.get()
                        mk.tt("dve", khat[:], kh0[:], gi[:, C_GK:C_GK + 256], ALU.mult)
                        gv = gvr.get()
                        mk.cp("pool", gv[:], gi[:, C_GV:C_GV + 512])
                        yield
                        if want_o:
                            qt = g256.get()
                            mk.act(qt[:], psC[:, 0:256], AF.Exp)
                            mk.stt("dve", qt[:], qt[:], 0.125, gi[:, C_GQ:C_GQ + 256], ALU.mult, ALU.mult)
                            kt = g256.get()
                            mk.act(kt[:], psC[:, 0:256], AF.Exp, scale=-1.0)
                            mk.tt("dve", kt[:], kt[:], gi[:, C_GK:C_GK + 256], ALU.mult)
                            yield
                            qk = []
                            for p_ in range(2):
                                psT = psGt.get()
                                mk.tr(psT[:, 0:128], qt[:, p_ * 128:(p_ + 1) * 128], ident[:])
                                mk.tr(psT[:, 128:256], kt[:, p_ * 128:(p_ + 1) * 128], ident[:])
                                q_ = qkT.get()
                                mk.cp("act", q_[:], psT[:, 0:256])
                                qk.append(q_)
                                yield
                            psO = psGo
                            for h in range(4):
                                p_, s_ = h // 2, h % 2
                                b0 = s_ * 64
                                psAt = psGt.get()
                                mk.mm(psAt[:, 0:128], qk[p_][b0:b0 + 64, 128:256], qk[p_][b0:b0 + 64, 0:128])
                                at = attS.get()
                                mk.tt("dve", at[:], psAt[:, 0:128], (M_LE if d_ == 0 else M_GE)[:], ALU.mult)
                                yield
                                mk.mm(psO[:, h * 128:(h + 1) * 128], at[:], gv[:, h * 128:(h + 1) * 128],
                                      start=True, stop=False)
                                mk.mm(psO[:, h * 128:(h + 1) * 128], qk[p_][b0:b0 + 64, 0:128],
                                      Sst_r[d_][p_][b0:b0 + 64, s_ * 128:(s_ + 1) * 128], start=False, stop=True)
                                yield
                            ot = ev.get()
                            if d_ == 0:
                                mk.cp("act", ot[:], psO[:])
                                mk.dma("pool", OF[r0_:r1_, :], ot[:], km={"OF_scr": [("OF", ti)]})
                            else:
                                of = ev.get()
                                mk.dma("sp", of[:], OF[r0_:r1_, :], km={"OF_scr": [("OF", ti)]})
                                mk.tt("dve", ot[:], psO[:], of[:], ALU.add)
                                yield
                                mk.tt("dve", of[:], ot[:], ot[:], ALU.mult)
                                s4 = st4.get()
                                mk.op("dve", lambda e, s4=s4, of=of: e.reduce_sum(out=s4[:], in_=of[:].rearrange("p (h v) -> p h v", h=4), axis=AX.X), [s4[:]], [of[:]])
                                mk.act(s4[:], s4[:], AF.Ln, bias=1e-6, scale=1.0 / 128)
                                mk.act(s4[:], s4[:], AF.Exp, scale=-0.5)
                                yield
                                for h in range(4):
                                    mk.stt("dve", ot[:, h * 128:(h + 1) * 128], ot[:, h * 128:(h + 1) * 128], s4[:, h:h + 1],
                                           gglab[:], ALU.mult, ALU.mult)
                                mk.act(of[:], gi[:, C_OG:C_OG + 512], AF.Silu)
                                mk.tt("dve", ot[:], ot[:], of[:], ALU.mult)
                                mk.dma("pool", YA[r0_:r1_, :], ot[:], km={"YA_scr": [("YA_scr", ti)]})
                        for p_ in range(2):
                            psS = psGt.get()
                            mk.mm(psS[:, 0:256], khat[:, p_ * 128:(p_ + 1) * 128], gv[:, p_ * 256:(p_ + 1) * 256])
                            mk.stt("dve", Sst[d_][p_][:], Sst[d_][p_][:], dc[:, p_:p_ + 1], psS[:, 0:256], ALU.mult, ALU.add)
                            mk.cp("act", Sst_r[d_][p_][:], Sst[d_][p_][:])
                            yield
                        yield
            def swa_gen():
                qTs = {}
                for ti in range(NT + 1):
                    if ti < NT:
                        r0_, r1_ = ti * 128, (ti + 1) * 128
                        si = sin_.get()
                        mk.dma("sp", si[:], P[r0_:r1_, C_SQ:C_SQ + 768], km={"P_scr": [("P", ti)]})
                        if ti >= NZ:
                            tx = (ti - NZ) * 128
                            rt = rope_t.get()
                            mk.dma("sp", rt[:, 0:64], ropeC[tx:tx + 128, :])
                            mk.dma("sp", rt[:, 64:128], ropeS[tx:tx + 128, :])
                            xv = si[:, 0:640].rearrange("p (h d) -> p h d", d=64)
                            t1 = rq.get()
                            t2 = rq.get()
                            mk.tt("dve", t1[:].rearrange("p (h d) -> p h d", d=64), xv,
                                  rt[:, 0:64].unsqueeze(1).broadcast_to([128, 10, 64]), ALU.mult)
                            x5 = si[:, 0:640].rearrange("p (h a s f) -> p h a s f", a=2, s=2, f=16)
                            o5 = t2[:].rearrange("p (h a s f) -> p h a s f", a=2, s=2, f=16)
                            s5 = rt[:, 64:128].rearrange("p (a s f) -> p a s f", a=2, s=2).unsqueeze(1).broadcast_to([128, 10, 2, 2, 16])
                            for s_ in range(2):
                                mk.tt("dve", o5[:, :, :, s_, :], x5[:, :, :, 1 - s_, :], s5[:, :, :, s_, :], ALU.mult)
                            mk.tt("pool", t1[:], t1[:], t2[:], ALU.add)
                            yield
                            qk_ap = t1
                        else:
                            qk_ap = si
                        qT = qTr.get()
                        psT = psSt.get()
                        for c_ in range(4):
                            mk.tr(psT[:, c_ * 128:(c_ + 1) * 128], qk_ap[:, c_ * 128:(c_ + 1) * 128], ident[:])
                        mk.cp("act", qT[:].rearrange("p c t -> p (c t)"), psT[:, 0:512])
                        yield
                        kdt = kd.get()
                        for kv in range(2):
                            for dup in range(2):
                                mk.cp("pool", kdt[:, kv * 128 + dup * 64:kv * 128 + dup * 64 + 64],
                                      qk_ap[:, 512 + kv * 64:512 + kv * 64 + 64])
                        psK = psSt.get()
                        for kv in range(2):
                            mk.tr(psK[:, kv * 128:(kv + 1) * 128], kdt[:, kv * 128:(kv + 1) * 128], ident[:])
                            mk.cp("act", KTd[kv][:, r0_:r1_], psK[:, kv * 128:(kv + 1) * 128], km={KTd[kv][:].name: [("KT", kv, ti)]})
                        mk.cp("dve", VX[:, ti, :], si[:, 640:768], km={VX[:].name: [("VX", ti)]})
                        qTs[ti] = qT
                    todo = []
                    if ti == NZ - 1 and need_ctx:
                        todo = [(z_, None) for z_ in range(NZ)]
                    if ti - 1 >= NZ:
                        tq = ti - 1
                        a_ = max(tq - 1, NZ)
                        b_ = min(tq + 1, NT - 1)
                        todo = [(tq, (a_, b_ - a_ + 1, 128 * (a_ - (tq - 1))))]
                    for (tq, loc) in todo:
                        kmq = {}
                        for kv in range(2):
                            ks = list(range(NZ)) + ([] if loc is None else list(range(loc[0], loc[0] + loc[1])))
                            kmq[KTd[kv][:].name] = [("KT", kv, x_) for x_ in ks]
                        kmq[VX[:].name] = [("VX", x_) for x_ in (list(range(NZ)) + ([] if loc is None else list(range(loc[0], loc[0] + loc[1]))))]
                        _op = mk.op

                        def op2(eng, fn, outs, ins, km=None, kmq=kmq):
                            _op(eng, fn, outs, ins, kmq if km is None else km)
                        mk.op = op2
                        yield from swa_queries(tq, qTs[tq], loc)
                        mk.op = _op
                    yield
            gen_a, gen_b = gla_gen(), swa_gen()
            alive = [True, True]
            while any(alive):
                for gi_, (g_, reps) in enumerate(((gen_a, 1), (gen_b, 1))):
                    for _ in range(reps):
                        if alive[gi_]:
                            try:
                                next(g_)
                            except StopIteration:
                                alive[gi_] = False
            mk.pop()
            if dbg == "SWA%d" % l:
                break
            mk.push()

            class RV8:
                def __init__(self, ts):
                    self.t, self.i = ts, 0

                def get(self):
                    t_ = self.t[self.i]
                    self.i = (self.i + 1) % len(self.t)
                    return t_
            psW = RV8([psA.t[0], psA.t[1], psA.t[2], psA.t[3], psB.t[0][:].bitcast(F32), psA.t[4], psA.t[5], psB.t[1][:].bitcast(F32)])
            LAM = 0.6065306597126334
            bcs = {}
            for nm_, src_ in (("kk", k_k[l:l + 1, :]), ("ka", k_a[l:l + 1, :]), ("rk", r_k[l:l + 1, :]),
                              ("gw", gn_w[l:l + 1, :]), ("gb", gn_b[l:l + 1, :]),
                              ("w00", w0[l, 0:1, :]), ("w01", w0[l, 1:2, :]), ("a00", a0[l, 0:1, :]), ("a01", a0[l, 1:2, :])):
                bcs[nm_] = mk.sb("bc_" + nm_, [128, 512])
                mk.dma("sp", bcs[nm_][:], src_.broadcast_to([128, 512]))
            ww2 = [mk.sb("ww2%d" % d_, [64, 512]) for d_ in range(2)]
            wa2 = [mk.sb("wa2%d" % d_, [128, 512]) for d_ in range(2)]
            wg2 = mk.sb("wg2", [128, 512])
            mk.dma("sp", wg2[:], w_g2[l])
            Sta = [mk.sb("Sta%d" % d_, [128, 512]) for d_ in range(2)]
            Sta_r = [mk.sb("Star%d" % d_, [128, 512], F32R) for d_ in range(2)]
            for d_ in range(2):
                mk.dma("sp", ww2[d_][:], w_w2[l, d_])
                mk.dma("sp", wa2[d_][64:128, :], w_a2[l, d_])
                mk.memset("pool", Sta[d_][:], 0.0)
                mk.cp("pool", Sta_r[d_][:], Sta[d_][:])
            rin = Ring(mk, "rin", 2, [128, 1792], F32)
            lrT = Ring(mk, "lrT", 4, [128, 128], F32)
            r5 = Ring(mk, "r5", 21, [128, 512], F32)
            FTr = Ring(mk, "FTr", 6, [128, 512], F32R)
            AXr = Ring(mk, "AXr", 9, [128, 512], F32R)
            wallR = Ring(mk, "wallR", 3, [128, 8, 384], F32R)
            rhallR = Ring(mk, "rhallR", 2, [128, 8, 128], F32R)
            bkR = Ring(mk, "bkR", 4, [128, 512], F32R)
            vrR = Ring(mk, "vrR", 2, [128, 512], F32R)
            aptR = Ring(mk, "aptR", 1, [128, 512], F32R)
            psbR = Ring(mk, "psbR", 2, [128, 512], F32R)

            s8r = Ring(mk, "s8r", 6, [128, 8], F32)
            maskX = []
            maskM = []
            for d_ in range(2):
                mx_ = mk.sb("maskX%d" % d_, [128, 512])
                strict, incl, mm_ = (BM["LT"], BM["LE"], BM["GT"]) if d_ == 0 else (BM["GT"], BM["GE"], BM["LT"])
                for i_, m_ in enumerate((strict, incl, strict, incl)):
                    mk.cp("pool", mx_[:, i_ * 128:(i_ + 1) * 128], m_[:])
                maskX.append(mx_)
                maskM.append(mm_)
            for d_ in range(2):
                order = list(range(NT)) if d_ == 0 else (list(range(NZ - 1, -1, -1)) + list(range(NT - 1, NZ - 1, -1)))
                Bincl, Bexcl, Brev = (BM["LE"], BM["LT"], BM["GT"]) if d_ == 0 else (BM["GE"], BM["GT"], BM["LT"])
                corder = (0, 1) if d_ == 0 else (1, 0)
                for ti in order:
                    want_o = need_ctx or ti >= NZ
                    r0_, r1_ = ti * 128, (ti + 1) * 128
                    ri = rin.get()
                    mk.dma("sp", ri[:], P[r0_:r1_, C_RR:C_GATE], km={"P_scr": [("P", ti)]})
                    r_, k_, v_ = ri[:, 0:512], ri[:, 512:1024], ri[:, 1024:1536]
                    psT = psW.get()
                    mk.tr(psT[:, 0:128], ri[:, 1536:1664], ident[:])
                    mk.tr(psT[:, 128:256], ri[:, 1664:1792], ident[:])
                    la = lrT.get()
                    mk.act(la[0:64, :], psT[0:64, 0:128], AF.Tanh)
                    mk.cp("dve", la[64:128, :], psT[64:128, 0:128])
                    kk = r5.get()
                    mk.tt("dve", kk[:], k_, bcs["kk"][:], ALU.mult)
                    sqv = r5.get()
                    mk.act(sqv[:], kk[:], AF.Square)
                    s8 = s8r.get()
                    mk.op("dve", lambda e, s8=s8, sqv=sqv: e.reduce_sum(out=s8[:], in_=sqv[:].rearrange("p (h k) -> p h k", h=8), axis=AX.X), [s8[:]], [sqv[:]])
                    mk.ts("dve", s8[:], s8[:], 1e-24, None, ALU.max)
                    mk.act(s8[:], s8[:], AF.Ln)
                    mk.act(s8[:], s8[:], AF.Exp, scale=-0.5)
                    mk.tt("dve", kk[:].rearrange("p (h k) -> p h k", h=8), kk[:].rearrange("p (h k) -> p h k", h=8),
                          s8[:, :].unsqueeze(2).broadcast_to([128, 8, 64]), ALU.mult)
                    if d_ == 0 and want_o:
                        lb = lrT.get()
                        mk.act(lb[:], psT[:, 128:256], AF.Sigmoid)
                        psG = psW.get()
                        mk.mm(psG[:], lb[:], wg2[:])
                        gg = r5.get()
                        mk.cp("act", gg[:], psG[:])
                        mk.dma("pool", RG[r0_:r1_, :], gg[:], km={"RG_scr": [("RG", ti)]})
                    psU = psW.get()
                    mk.mm(psU[:], la[0:64, :], ww2[d_][0:64, :])
                    sg = r5.get()
                    mk.tt("dve", sg[:], psU[:], bcs["w0%d" % d_][:], ALU.add)
                    mk.act(sg[:], sg[:], AF.Sigmoid)
                    psU = psW.get()
                    mk.mm(psU[:], la[64:128, :], wa2[d_][64:128, :])
                    aa = r5.get()
                    mk.tt("dve", aa[:], psU[:], bcs["a0%d" % d_][:], ALU.add)
                    mk.act(aa[:], aa[:], AF.Sigmoid)
                    km_ = r5.get()
                    mk.stt("dve", km_[:], aa[:], -1.0, bcs["ka"][:], ALU.add, ALU.mult)
                    mk.stt("dve", km_[:], km_[:], 1.0, k_, ALU.add, ALU.mult)
                    bp = r5.get()
                    mk.tt("dve", bp[:], kk[:], aa[:], ALU.mult)
                    if d_ == 0:
                        if want_o:
                            mk.dma("pool", KF[r0_:r1_, :], km_[:], km={"KF_scr": [("KF", ti)]})
                    psC = psW.get()
                    mk.mm(psC[:], Bincl[:], sg[:])
                    Ei = r5.get()
                    mk.act(Ei[:], psC[:], AF.Exp, scale=-LAM)
                    Em = r5.get()
                    mk.act(Em[:], psC[:], AF.Exp, scale=LAM)
                    psC = psW.get()
                    mk.mm(psC[:], Bexcl[:], sg[:])
                    Ee = r5.get()
                    mk.act(Ee[:], psC[:], AF.Exp, scale=-LAM)
                    psC = psW.get()
                    mk.mm(psC[:], Brev[:], sg[:])
                    Er = r5.get()
                    mk.act(Er[:], psC[:], AF.Exp, scale=-LAM)
                    psD = psW.get()
                    for p_ in range(4):
                        mk.mm(psD[:, 2 * p_:2 * p_ + 2], sg[:, p_ * 128:(p_ + 1) * 128], cind[:, 0:2])
                    dcs = s8r.get()
                    mk.act(dcs[:], psD[:, 0:8], AF.Exp, scale=-LAM)
                    At = Ee
                    mk.stt("dve", At[:], kk[:], -1.0, Ee[:], ALU.mult, ALU.mult)
                    Rt = Ei
                    mk.tt("pool", Rt[:], r_, Ei[:], ALU.mult)
                    Bt = r5.get()
                    mk.tt("dve", Bt[:], bp[:], Em[:], ALU.mult)
                    Kt = Em
                    mk.tt("dve", Kt[:], km_[:], Em[:], ALU.mult)
                    Bh = bkR.get()
                    mk.tt("pool", Bh[:], bp[:], Er[:], ALU.mult)
                    Kh = bkR.get()
                    mk.tt("dve", Kh[:], km_[:], Er[:], ALU.mult)
                    v_r = vrR.get()
                    mk.cp("act", v_r[:], v_)
                    yt = r5.get()
                    FTs = []
                    for p_ in range(4):
                        pc = slice(p_ * 128, (p_ + 1) * 128)
                        psF = psW.get()
                        for i_, src_ in enumerate((At, Rt, Bt, Kt)):
                            mk.tr(psF[:, i_ * 128:(i_ + 1) * 128], src_[:, pc], ident[:])
                        FT = FTr.get()
                        mk.cp("act", FT[:], psF[:])
                        FTs.append(FT)
                    AXs = []
                    allk = lambda T_: {T_[:].name: [(T_[:].name, h_) for h_ in range(8)]}
                    Wall = wallR.get()
                    for h in range(8):
                        p_, b0 = h // 2, (h % 2) * 64
                        FT = FTs[p_]
                        kh_ = {Wall[:].name: [(Wall[:].name, h)]}
                        psX = psW.get()
                        mk.mm(psX[:, 0:256], FT[b0:b0 + 64, 256:384], FT[b0:b0 + 64, 0:256])
                        mk.mm(psX[:, 256:512], FT[b0:b0 + 64, 384:512], FT[b0:b0 + 64, 0:256])
                        AX_ = AXr.get()
                        mk.tt("dve", AX_[:], psX[:], maskX[d_][:], ALU.mult)
                        AXs.append(AX_)
                        psY = psW.get()
                        mk.mm(psY[:, 0:128], FT[b0:b0 + 64, 0:128], FT[b0:b0 + 64, 256:384])
                        mk.tt("dve", Wall[:, h, 128:256], psY[:, 0:128], maskM[d_][:], ALU.mult, km=kh_)
                        mk.cp("act", Wall[:, h, 256:384], AX_[:, 0:128].bitcast(F32), km=kh_)
                    psV = psW.get()
                    for h in range(8):
                        mk.mm(psV[:, h * 64:(h + 1) * 64], AXs[h][:, 256:384], v_r[:, h * 64:(h + 1) * 64])
                    mk.cp("pool", Wall[:, :, 0:64], At[:].rearrange("p (h k) -> p h k", h=8), km=allk(Wall))
                    mk.cp("act", Wall[:, :, 64:128], psV[:].rearrange("p (h k) -> p h k", h=8), km=allk(Wall))
                    for j in range(6):
                        lastj = (j == 5)
                        Wn = rhallR.get() if lastj else wallR.get()
                        for h in range(8):
                            kmh = {Wall[:].name: [(Wall[:].name, h)], Wn[:].name: [(Wn[:].name, h)]}
                            psR = psW.get()
                            if not lastj:
                                mk.mm(psR[:, 0:256], Wall[:, h, 256:384], Wall[:, h, 0:256], km=kmh)
                                mk.mm(psR[:, 256:384], Wall[:, h, 128:256], Wall[:, h, 256:384], km=kmh)
                                mk.tt("dve", Wn[:, h, 0:128], psR[:, 0:128], Wall[:, h, 0:128].bitcast(F32), ALU.add, km=kmh)
                                mk.cp("act", Wn[:, h, 128:384], psR[:, 128:384], km=kmh)
                            else:
                                mk.mm(psR[:, 0:128], Wall[:, h, 256:384], Wall[:, h, 0:128], km=kmh)
                                mk.tt("dve", Wn[:, h, :], psR[:, 0:128], Wall[:, h, 0:128].bitcast(F32), ALU.add, km=kmh)
                        Wall = Wn
                    RHall = Wall
                    ApAll = r5.get()
                    mk.cp("dve", ApAll[:].rearrange("p (h k) -> p h k", h=8), RHall[:, :, 0:64].bitcast(F32), km=allk(RHall))
                    psZ = psW.get()
                    for p_ in range(4):
                        mk.tr(psZ[:, p_ * 128:(p_ + 1) * 128], ApAll[:, p_ * 128:(p_ + 1) * 128], ident[:])
                    ApT = aptR.get()
                    mk.cp("act", ApT[:], psZ[:])
                    Psb = psbR.get()
                    St = Sta[d_]
                    Str = Sta_r[d_]
                    Uv = RHall[:].bitcast(F32).rearrange("p (pp s) c -> p pp s c", s=2)
                    Pv = Psb[:].rearrange("p (pp s k) -> p pp s k", s=2, k=64)
                    Yv = yt[:].rearrange("p (pp s k) -> p pp s k", s=2, k=64)
                    dv = dcs[:].rearrange("p (pp c) -> p pp c", c=2)
                    for c_ in corder:
                        r_lo, r_hi = c_ * 64, (c_ + 1) * 64
                        psP = [psW.get(), psW.get()]
                        for s_ in range(2):
                            b0 = s_ * 64
                            for p_ in range(4):
                                stb = Str[b0:b0 + 64, p_ * 128 + b0:p_ * 128 + b0 + 64]
                                mk.mm(psP[s_][:, p_ * 128:p_ * 128 + 64], ApT[b0:b0 + 64, p_ * 128:(p_ + 1) * 128], stb)
                                if want_o:
                                    mk.mm(psP[s_][:, p_ * 128 + 64:(p_ + 1) * 128], FTs[p_][b0:b0 + 64, 128:256], stb)
                        for s_ in range(2):
                            pv = psP[s_][:].rearrange("p (pp c) -> p pp c", c=128)
                            mk.tt("dve", Pv[r_lo:r_hi, :, s_, :], pv[r_lo:r_hi, :, 0:64], Uv[r_lo:r_hi, :, s_, 64:128], ALU.add,
                                  km=allk(RHall))
                            if want_o:
                                mk.cp("act", Yv[r_lo:r_hi, :, s_, :], pv[r_lo:r_hi, :, 64:128])
                        psS = psW.get()
                        for p_ in range(4):
                            pc = slice(p_ * 128, (p_ + 1) * 128)
                            mk.mm(psS[:, pc], Bh[r_lo:r_hi, pc], Psb[r_lo:r_hi, pc], start=True, stop=False)
                            mk.mm(psS[:, pc], Kh[r_lo:r_hi, pc], v_r[r_lo:r_hi, pc], start=False, stop=True)
                        mk.tt("dve", St[:].rearrange("p (pp c) -> p pp c", c=128), St[:].rearrange("p (pp c) -> p pp c", c=128),
                              dv[:, :, c_:c_ + 1].broadcast_to([128, 4, 128]), ALU.mult)
                        mk.tt("dve", St[:], psS[:], St[:], ALU.add)
                        mk.cp("act", Str[:], St[:])
                    if want_o:
                        psQ = psW.get()
                        for h in range(8):
                            hc = slice(h * 64, (h + 1) * 64)
                            mk.mm(psQ[:, hc], AXs[h][:, 128:256], Psb[:, hc], start=True, stop=False)
                            mk.mm(psQ[:, hc], AXs[h][:, 384:512], v_r[:, hc], start=False, stop=True)
                        mk.tt("dve", yt[:], psQ[:], yt[:], ALU.add)
                    if not want_o:
                        continue
                    if d_ == 0:
                        mk.dma("pool", YF[r0_:r1_, :], yt[:], km={"YF_scr": [("YF", ti)]})
                    else:
                        yf = r5.get()
                        mk.dma("sp", yf[:], YF[r0_:r1_, :], km={"YF_scr": [("YF", ti)]})
                        kf = r5.get()
                        mk.dma("sp", kf[:], KF[r0_:r1_, :], km={"KF_scr": [("KF", ti)]})
                        gg = r5.get()
                        mk.dma("sp", gg[:], RG[r0_:r1_, :], km={"RG_scr": [("RG", ti)]})
                        mk.tt("dve", yt[:], yt[:], yf[:], ALU.add)
                        y3 = yt[:].rearrange("p (h k) -> p h k", h=8)
                        mu8 = s8r.get()
                        mk.op("dve", lambda e, mu8=mu8, y3=y3: e.reduce_sum(out=mu8[:], in_=y3, axis=AX.X), [mu8[:]], [yt[:]])
                        mk.ts("dve", mu8[:], mu8[:], 1.0 / 64)
                        mk.tt("dve", y3, y3, mu8[:, :].unsqueeze(2).broadcast_to([128, 8, 64]), ALU.subtract)
                        mk.tt("pool", yf[:], yt[:], yt[:], ALU.mult)
                        v8 = s8r.get()
                        mk.op("dve", lambda e, v8=v8, yf=yf: e.reduce_sum(out=v8[:], in_=yf[:].rearrange("p (h k) -> p h k", h=8), axis=AX.X), [v8[:]], [yf[:]])
                        mk.act(v8[:], v8[:], AF.Ln, bias=64e-5, scale=1.0 / 64)
                        mk.act(v8[:], v8[:], AF.Exp, scale=-0.5)
                        mk.tt("dve", y3, y3, v8[:, :].unsqueeze(2).broadcast_to([128, 8, 64]), ALU.mult)
                        mk.tt("dve", yt[:], yt[:], bcs["gw"][:], ALU.mult)
                        mk.tt("dve", yt[:], yt[:], bcs["gb"][:], ALU.add)
                        mk.tt("dve", kf[:], kf[:], km_[:], ALU.add)
                        mk.tt("dve", kf[:], kf[:], r_, ALU.mult)
                        mk.tt("dve", kf[:], kf[:], bcs["rk"][:], ALU.mult)
                        b8 = s8r.get()
                        mk.op("dve", lambda e, b8=b8, kf=kf: e.reduce_sum(out=b8[:], in_=kf[:].rearrange("p (h k) -> p h k", h=8), axis=AX.X), [b8[:]], [kf[:]])
                        mk.tt("dve", kf[:].rearrange("p (h k) -> p h k", h=8), v_.rearrange("p (h k) -> p h k", h=8),
                              b8[:, :].unsqueeze(2).broadcast_to([128, 8, 64]), ALU.mult)
                        mk.tt("dve", yt[:], yt[:], kf[:], ALU.add)
                        mk.tt("dve", yt[:], yt[:], gg[:], ALU.mult)
                        mk.dma("pool", YC[r0_:r1_, :], yt[:], km={"YC_scr": [("YC_scr", ti)]})
            mk.pop()
            if dbg == "RWKV%d" % l:
                break
            mk.push()
            xsrc = (lambda ti: xz[ti * 128:(ti + 1) * 128, :]) if l == 0 else (lambda ti: XS[ti * 128:(ti + 1) * 128, :])
            gab0 = [mk.sb("gab0%d" % w, [128, D]) for w in range(2)]
            for w in range(2):
                mk.dma("sp", gab0[w][:], MODROW[w:w + 1, 16 * 128:16 * 128 + D].broadcast_to([128, D]))
            stg = Ring(mk, "stg", 2, [128, 2, 1024], F32)
            Wb = mk.sb("Wb", [128, 12, 1024], BF16)
            Wo = mk.sb("Wo", [128, 8, 1024], BF16)
            wbv = w_branch[l].rearrange("i (k p) n -> p (i k) n", p=128)
            wov = w_out[l].rearrange("(k p) n -> p k n", p=128)
            for c_ in range(6):
                sg_ = stg.get()
                mk.dma("sp", sg_[:], wbv[:, 2 * c_:2 * c_ + 2, :])
                mk.cp("pool", Wb[:, 2 * c_:2 * c_ + 2, :], sg_[:])
            for c_ in range(4):
                sg_ = stg.get()
                mk.dma("sp", sg_[:], wov[:, 2 * c_:2 * c_ + 2, :])
                mk.cp("pool", Wo[:, 2 * c_:2 * c_ + 2, :], sg_[:])
            w13s = Ring(mk, "w13s", 2, [128, 2, 8, 128], F32)
            w13c = Ring(mk, "w13c", 2, [128, 2, 8, 128], BF16)
            w1v = w_ffn1[l].rearrange("(k p) n -> p k n", p=128)
            w3v = w_ffn3[l].rearrange("(k p) n -> p k n", p=128)
            def conv_gen():
                for fc in range(22):
                    ws = w13s.get()
                    mk.dma("sp", ws[:, 0, :, :], w1v[:, :, fc * 128:(fc + 1) * 128])
                    mk.dma("sp", ws[:, 1, :, :], w3v[:, :, fc * 128:(fc + 1) * 128])
                    wc = w13c.get()
                    mk.cp("dve", wc[:, 0, :, :], ws[:, 0, :, :])
                    mk.cp("act", wc[:, 1, :, :], ws[:, 1, :, :])
                    mk.dma("pool", W13B[fc], wc[:].rearrange("p a k f -> p (a k f)"), km={"W13B_scr": [("W13B", fc)]})
                    yield
            cgen = conv_gen()
            yin = Ring(mk, "yin", 2, [128, 1536], F32)
            gin_ = Ring(mk, "gin_", 2, [128, 3072], F32)
            yTr = Ring(mk, "yTr", 2, [128, 12, 128], BF16)
            mTr = Ring(mk, "mTr", 2, [128, 8, 128], BF16)
            mS = Ring(mk, "mS", 2, [128, 1024], F32)
            xin2 = Ring(mk, "xin2", 2, [128, 1024], F32)
            ev2 = Ring(mk, "ev2", 4, [128, 512], F32)
            for ti in range(NT):
                if ti < NZ and not need_ctx:
                    continue
                w = 1 if ti < NZ else 0
                r0_, r1_ = ti * 128, (ti + 1) * 128
                next(cgen, None)
                yi = yin.get()
                for i_, Y_ in enumerate((YA, YB, YC)):
                    mk.dma("sp", yi[:, i_ * 512:(i_ + 1) * 512], Y_[r0_:r1_, :], km={Y_.name: [(Y_.name, ti)]})
                gi = gin_.get()
                mk.dma("sp", gi[:], P[r0_:r1_, C_GATE:C_GATE + 3072], km={"P_scr": [("P", ti)]})
                mk.act(gi[:], gi[:], AF.Sigmoid)
                yT = yTr.get()
                for g3 in range(3):
                    psT = psA.get()
                    for c_ in range(4):
                        mk.tr(psT[:, c_ * 128:(c_ + 1) * 128], yi[:, g3 * 512 + c_ * 128:g3 * 512 + (c_ + 1) * 128], ident[:])
                    mk.cp("act", yT[:, g3 * 4:(g3 + 1) * 4, :].rearrange("p c t -> p (c t)"), psT[:])
                m_ = mS.get()
                for i_ in range(3):
                    for half in range(2):
                        ps = psA.get()
                        for k in range(4):
                            mk.mm(ps[:], yT[:, i_ * 4 + k, :], Wb[:, i_ * 4 + k, half * 512:(half + 1) * 512],
                                  start=(k == 0), stop=(k == 3))
                        gsl = gi[:, i_ * 1024 + half * 512:i_ * 1024 + (half + 1) * 512]
                        if i_ == 0:
                            mk.tt("dve", m_[:, half * 512:(half + 1) * 512], ps[:], gsl, ALU.mult)
                        else:
                            e_ = ev2.get()
                            mk.tt("dve", e_[:], ps[:], gsl, ALU.mult)
                            mk.tt("pool", m_[:, half * 512:(half + 1) * 512], m_[:, half * 512:(half + 1) * 512], e_[:], ALU.add)
                mT = mTr.get()
                for half in range(2):
                    psT = psA.get()
                    for c_ in range(4):
                        mk.tr(psT[:, c_ * 128:(c_ + 1) * 128], m_[:, (half * 4 + c_) * 128:(half * 4 + c_ + 1) * 128], ident[:])
                    mk.cp("act", mT[:, half * 4:(half + 1) * 4, :].rearrange("p c t -> p (c t)"), psT[:])
                xt = xin2.get()
                mk.dma("sp", xt[:], xsrc(ti), km={xsrc(ti).name: [(xsrc(ti).name, ti)]})
                for half in range(2):
                    ps = psA.get()
                    for k in range(8):
                        mk.mm(ps[:], mT[:, k, :], Wo[:, k, half * 512:(half + 1) * 512], start=(k == 0), stop=(k == 7))
                    e_ = ev2.get()
                    mk.tt("dve", e_[:], ps[:], gab0[w][:, half * 512:(half + 1) * 512], ALU.mult)
                    mk.tt("pool", xt[:, half * 512:(half + 1) * 512], xt[:, half * 512:(half + 1) * 512], e_[:], ALU.add)
                mk.dma("pool", XS[r0_:r1_, :], xt[:], km={"X_scr": [("X_scr", ti)]})
            for _ in cgen:
                pass
            mk.pop()
            if dbg == "MRG%d" % l:
                break
            mk.push()
            last = (l == L - 1)
            W2 = mk.sb("W2", [128, 22, 1024], BF16)
            stg = Ring(mk, "stg", 2, [128, 2, 1024], F32)
            w2v = w_ffn2[l].rearrange("(k p) n -> p k n", p=128)
            for c_ in range(11):
                sg_ = stg.get()
                mk.dma("sp", sg_[:], w2v[:, 2 * c_:2 * c_ + 2, :])
                mk.cp("pool", W2[:, 2 * c_:2 * c_ + 2, :], sg_[:])
            gab1 = [mk.sb("gab1%d" % w, [128, D]) for w in range(2)]
            for w in range(2):
                mk.dma("sp", gab1[w][:], MODROW[w:w + 1, 40 * 128:40 * 128 + D].broadcast_to([128, D]))
            gfb = mk.sb("gfb", [128, 1024])
            mk.dma("sp", gfb[:], g_final[0:1, :].broadcast_to([128, 1024]))
            xin = Ring(mk, "xin", 5, [128, D], F32)
            sq = mk.sb("sqjunk", [128, D])
            st1 = Ring(mk, "st1", 4, [128, 1], F32)
            dg = Ring(mk, "dg", 2, [128, 128], F32)
            hTb = mk.sb("hTb", [128, 8, 512], BF16)
            actT = mk.sb("actT", [128, 22, 512], BF16)
            w13b = Ring(mk, "w13b", 3, [128, 2, 8, 128], BF16)
            u1s = Ring(mk, "u1s", 2, [128, 512], F32)
            ev2 = Ring(mk, "ev2", 4, [128, 512], F32)
            w1v = w_ffn1[l].rearrange("(k p) n -> p k n", p=128)
            w3v = w_ffn3[l].rearrange("(k p) n -> p k n", p=128)
            tiles = [t_ for t_ in range(NT) if (t_ >= NZ or need_ctx)]
            blocks = [tiles[i_:i_ + 4] for i_ in range(0, len(tiles), 4)]
            for blk in blocks:
                nb = len(blk)
                wdt = nb * 128
                xts = []
                for bi, ti in enumerate(blk):
                    w = 1 if ti < NZ else 0
                    xt = xin.get()
                    xts.append(xt)
                    mk.dma("sp", xt[:], XS[ti * 128:(ti + 1) * 128, :], km={"X_scr": [("X_scr", ti)]})
                    ss = st1.get()
                    mk.act(sq[:], xt[:], AF.Square, accum=ss[:])
                    mk.act(ss[:], ss[:], AF.Ln, bias=1e-6, scale=1.0 / D)
                    mk.act(ss[:], ss[:], AF.Exp, scale=-0.5)
                    dgt = dg.get()
                    mk.ts("dve", dgt[:], ident[:], ss[:, 0:1])
                    for half in range(2):
                        ps = psA.get()
                        for kk in range(4):
                            k = half * 4 + kk
                            mk.mm(ps[:, kk * 128:(kk + 1) * 128], xt[:, k * 128:(k + 1) * 128], dgt[:])
                        for kk in range(4):
                            k = half * 4 + kk
                            mk.act(hTb[:, k, bi * 128:(bi + 1) * 128], ps[:, kk * 128:(kk + 1) * 128], AF.Identity,
                                   bias=modsb[:, 24 + k, w:w + 1], scale=A2[:, k, w:w + 1])
                for fc in range(22):
                    wb_ = w13b.get()
                    mk.dma("sp", wb_[:].rearrange("p a k f -> p (a k f)"), W13B[fc], km={"W13B_scr": [("W13B", fc)]})
                    ps1 = psA.get()
                    for k in range(8):
                        mk.mm(ps1[:, 0:wdt], wb_[:, 0, k, :], hTb[:, k, 0:wdt], start=(k == 0), stop=(k == 7))
                    ps3 = psA.get()
                    for k in range(8):
                        mk.mm(ps3[:, 0:wdt], wb_[:, 1, k, :], hTb[:, k, 0:wdt], start=(k == 0), stop=(k == 7))
                    u1 = u1s.get()
                    mk.act(u1[:, 0:wdt], ps1[:, 0:wdt], AF.Silu)
                    mk.tt("dve", actT[:, fc, 0:wdt], ps3[:, 0:wdt], u1[:, 0:wdt], ALU.mult)
                for bi, ti in enumerate(blk):
                    w = 1 if ti < NZ else 0
                    xt = xts[bi]
                    for half in range(2):
                        ps = psA.get()
                        for k in range(22):
                            mk.mm(ps[:], actT[:, k, bi * 128:(bi + 1) * 128], W2[:, k, half * 512:(half + 1) * 512],
                                  start=(k == 0), stop=(k == 21))
                        e_ = ev2.get()
                        mk.tt("dve", e_[:], ps[:], gab1[w][:, half * 512:(half + 1) * 512], ALU.mult)
                        mk.tt("pool", xt[:, half * 512:(half + 1) * 512], xt[:, half * 512:(half + 1) * 512], e_[:], ALU.add)
                    if not last:
                        mk.dma("pool", XS[ti * 128:(ti + 1) * 128, :], xt[:], km={"X_scr": [("X_scr", ti)]})
                    else:
                        ss = st1.get()
                        mk.act(sq[:], xt[:], AF.Square, accum=ss[:])
                        mk.act(ss[:], ss[:], AF.Ln, bias=1e-6, scale=1.0 / D)
                        mk.act(ss[:], ss[:], AF.Exp, scale=-0.5)
                        mk.stt("dve", xt[:], xt[:], ss[:, 0:1], gfb[:], ALU.mult, ALU.mult)
                        tx = (ti - NZ) * 128
                        mk.dma("pool", yout[tx:tx + 128, :], xt[:], km={"yout": [("yout", ti)]})
                        if dbg is not None:
                            mk.dma("pool", XS[ti * 128:(ti + 1) * 128, :], xt[:], km={"X_scr": [("X_scr", ti)]})
            mk.pop()
            if dbg == "FFN%d" % l:
                break
        if dbg is not None and not NODUMP:
            S.barrier()
            for ti in range(NT):
                if dbg.startswith("P"):
                    for c0 in range(0, INW, 1024):
                        c1 = min(INW, c0 + 1024)
                        mk.dma("sp", dbg_out[ti * 128:(ti + 1) * 128, c0:c1], P[ti * 128:(ti + 1) * 128, c0:c1],
                               km={"P_scr": [("P", ti)]})
                else:
                    for i, Y in enumerate((YA, YB, YC)):
                        mk.dma("sp", dbg_out[ti * 128:(ti + 1) * 128, i * 512:(i + 1) * 512],
                               Y[ti * 128:(ti + 1) * 128, :], km={Y.name: [(Y.name, ti)]})
                    mk.dma("sp", dbg_out[ti * 128:(ti + 1) * 128, 2048:3072],
                           XS[ti * 128:(ti + 1) * 128, :], km={"X_scr": [("X_scr", ti)]})
        S.emit()
    return nc


def host_inputs(inputs, b, NZ=2, NX=32, GRID_W=64):
    f = lambda a: np.ascontiguousarray(np.asarray(a, dtype=np.float32))
    m = {}
    m["xz"] = f(np.concatenate([inputs["ctx"][b], inputs["x"][b]], axis=0))
    c2 = np.stack([np.asarray(inputs["c"][b]), np.asarray(inputs["c_ctx"])], axis=-1)
    m["cT"] = f(c2.reshape(8, 128, 2).transpose(1, 0, 2))
    L = inputs["w_ada"].shape[0]
    m["b_adaT"] = f(np.asarray(inputs["b_ada"]).reshape(L, 48, 128).transpose(0, 2, 1))
    m["g_mixT"] = f(np.asarray(inputs["g_mix"]).reshape(L, 8, 128).transpose(0, 2, 1))
    m["g_ffnT"] = f(np.asarray(inputs["g_ffn"]).reshape(L, 8, 128).transpose(0, 2, 1))
    for k in ("w_ada", "w_in", "w_gk2", "b_gk", "g_gla", "sink", "mu_shift", "w0", "w_w2", "a0", "w_a2", "w_g2",
              "k_k", "k_a", "gn_w", "gn_b", "w_branch", "w_out", "w_ffn1", "w_ffn3", "w_ffn2"):
        m[k] = f(inputs[k])
    m["r_k"] = f(np.asarray(inputs["r_k"]).reshape(L, 512))
    m["g_final"] = f(np.asarray(inputs["g_final"]).reshape(1, D))
    T = NX * 128
    t = np.arange(T)
    row = (t // GRID_W).astype(np.float32)
    col = (t % GRID_W).astype(np.float32)
    inv = (np.float32(10000.0) ** (-np.arange(16, dtype=np.float32) / np.float32(16))).astype(np.float32)
    ar = (row[:, None] * inv[None, :]).astype(np.float32)
    ac = (col[:, None] * inv[None, :]).astype(np.float32)
    cosT = np.concatenate([np.cos(ar), np.cos(ar), np.cos(ac), np.cos(ac)], axis=1)
    sinT = np.concatenate([-np.sin(ar), np.sin(ar), -np.sin(ac), np.sin(ac)], axis=1)
    m["ropeC"] = f(cosT)
    m["ropeS"] = f(sinT)
    return m


_NC_CACHE = {}


def kernel(**inputs):
    B = inputs["x"].shape[0]
    if "nc" not in _NC_CACHE:
        _NC_CACHE["nc"] = build()
    nc = _NC_CACHE["nc"]
    maps = [host_inputs(inputs, b % B) for b in range(8)]
    res = run_bass_kernel_spmd(nc, maps, core_ids=list(range(8)))
    out = np.stack([res.results[b]["yout"] for b in range(B)], axis=0)
    return out.astype(np.float32)
```

```python
import contextlib
import numpy as np
import concourse.bass as bass
import concourse.mybir as mybir
from concourse.bass_utils import run_bass_kernel_spmd

F32 = mybir.dt.float32
BF16 = mybir.dt.bfloat16
F32R = mybir.dt.float32r
AF = mybir.ActivationFunctionType
ALU = mybir.AluOpType
AX = mybir.AxisListType

D = 1024
INW = 7200
FH = 2816
ENGS = ("pe", "act", "dve", "pool", "sp")
DMA_Q = {"sp": 6, "pool": 4, "act": 2}


class Sched:
    def __init__(self, nc):
        self.nc = nc
        self.q = {e: [] for e in ENGS}
        self.cnt = {e: 0 for e in ENGS}
        self.seen = {e: {} for e in ENGS}
        self.buf = {}
        self.slot = {qe: 0 for qe in DMA_Q}
        self.vengs = []
        for qe, n in DMA_Q.items():
            for i in range(n):
                v = "%s_d%d" % (qe, i)
                self.vengs.append(v)
                self.cnt[v] = 0
        self.sems = {}
        self.ninstr = 0
        self.snaps = {}

    def _deps(self, reads, writes):
        deps = {}

        def need(f, n):
            if n > deps.get(f, 0):
                deps[f] = n

        for k in reads:
            b = self.buf.get(k)
            if b and b["w"]:
                need(*b["w"])
        for k in writes:
            b = self.buf.get(k)
            if b:
                if b["w"]:
                    need(*b["w"])
                for f, n in b["r"].items():
                    need(f, n)
        return deps

    def _waits(self, eng, deps):
        waits = []
        know = self.seen[eng]
        for f, n in sorted(deps.items(), key=lambda kv: -kv[1]):
            if n <= 0:
                continue
            if f == eng:
                if eng == "pe":
                    continue
                if eng in ("dve", "act") and n < self.cnt[eng]:
                    continue
            if know.get(f, 0) >= n:
                continue
            know[f] = n
            snap = self.snaps.get(f)
            if snap is not None and n - 1 < len(snap):
                for k2, v2 in snap[n - 1]:
                    if v2 > know.get(k2, 0):
                        know[k2] = v2
            waits.append((f, n))
        return waits

    def _snap(self, who, issuer):
        self.snaps.setdefault(who, []).append(tuple(self.seen[issuer].items()))

    def _record(self, who, n, reads, writes):
        for k in reads:
            b = self.buf.setdefault(k, {"w": None, "r": {}})
            b["r"][who] = n
        for k in writes:
            self.buf[k] = {"w": (who, n), "r": {}}

    def add(self, eng, fn, reads=(), writes=()):
        deps = self._deps(reads, writes)
        waits = self._waits(eng, deps)
        self.cnt[eng] += 1
        n = self.cnt[eng]
        self._snap(eng, eng)
        self.q[eng].append((waits, fn, eng))
        self._record(eng, n, reads, writes)
        self.ninstr += 1

    def dma(self, qe, out, in_, reads=(), writes=(), **kw):
        s = self.slot[qe]
        self.slot[qe] = (s + 1) % DMA_Q[qe]
        v = "%s_d%d" % (qe, s)
        deps = self._deps(reads, writes)
        if self.cnt[v] > deps.get(v, 0):
            deps[v] = self.cnt[v]
        waits = self._waits(qe, deps)
        self.cnt[v] += 1
        n = self.cnt[v]
        self._snap(v, qe)

        def fn(e, out=out, in_=in_, kw=kw):
            return e.dma_start(out=out, in_=in_, **kw)

        self.q[qe].append((waits, fn, v))
        self._record(v, n, reads, writes)
        self.ninstr += 1

    def barrier(self):
        allc = dict(self.cnt)
        for e in ENGS:
            w = self._waits(e, {f: n for f, n in allc.items() if f != e})
            if w:
                self.q[e].append((w, None, None))

    def emit(self):
        nc = self.nc
        with contextlib.ExitStack() as st:
            for e in list(ENGS) + self.vengs:
                self.sems[e] = st.enter_context(nc.semaphore("s_" + e))
            block = st.enter_context(nc.Block())

            def val(f, n):
                return n * 16 if f in self.vengs else n

            def run(engname, e):
                for waits, fn, who in self.q[engname]:
                    if fn is None:
                        for f, n in waits:
                            e.wait_ge(self.sems[f], val(f, n))
                        continue
                    for f, n in waits[:-1]:
                        e.wait_ge(self.sems[f], val(f, n))
                    ins = fn(e)
                    if waits:
                        f, n = waits[-1]
                        ins._wait_ge(self.sems[f], val(f, n))
                    ins.then_inc(self.sems[who], 16 if who in self.vengs else 1)
                if engname == "sp":
                    for v in self.vengs:
                        if self.cnt[v] > 0:
                            e.wait_ge(self.sems[v], 16 * self.cnt[v])
                    for f in ENGS:
                        if f != "sp" and self.cnt[f] > 0:
                            e.wait_ge(self.sems[f], self.cnt[f])

            @block.tensor
            def _(e):
                run("pe", e)

            @block.scalar
            def _(e):
                run("act", e)

            @block.vector
            def _(e):
                run("dve", e)

            @block.gpsimd
            def _(e):
                run("pool", e)

            @block.sync
            def _(e):
                run("sp", e)


class Ring:
    def __init__(self, mk, name, n, shape, dt=F32, psum=False):
        self.t = []
        for i in range(n):
            nm = "%s%d_%d" % (name, i, mk.uid)
            if psum:
                self.t.append(mk.st.enter_context(mk.nc.psum_tensor(nm, shape, dt)))
            else:
                self.t.append(mk.st.enter_context(mk.nc.sbuf_tensor(nm, shape, dt)))
        self.i = 0

    def get(self):
        t = self.t[self.i]
        self.i = (self.i + 1) % len(self.t)
        return t


class MK:
    def __init__(self, nc, st):
        self.nc = nc
        self.st = st
        self.S = Sched(nc)
        self.uid = 0
        self.stack = []

    def push(self):
        self.stack.append(self.st)
        self.st = contextlib.ExitStack()
        self.st.__enter__()
        self.uid += 1

    def pop(self):
        self.S.barrier()
        self.st.close()
        self.st = self.stack.pop()

    def sb(self, name, shape, dt=F32):
        return self.st.enter_context(self.nc.sbuf_tensor("%s_%d" % (name, self.uid), shape, dt))

    def _k(self, aps, km):
        out = []
        for a in aps:
            if a is None or isinstance(a, (int, float)):
                continue
            if isinstance(a, (tuple, str)):
                out.append(a)
                continue
            n = a.name
            if km and n in km:
                out.extend(km[n])
            else:
                out.append(n)
        return out

    def op(self, eng, fn, outs, ins, km=None):
        rk = self._k(ins, km)
        wk = self._k(outs, km)
        ps = [k for k in rk if isinstance(k, str) and k.startswith("ps")]
        if ps:
            rk = [k for k in rk if k not in ps]
            wk = wk + [k for k in ps if k not in wk]
        self.S.add(eng, fn, reads=rk, writes=wk)

    def mm(self, out, lhsT, rhs, start=True, stop=True, km=None):
        self.op("pe", lambda e: e.matmul(out, lhsT, rhs, start=start, stop=stop), [out], [lhsT, rhs], km)

    def tr(self, out, in_, ident, km=None):
        self.op("pe", lambda e: e.transpose(out, in_, ident), [out], [in_, ident], km)

    def act(self, out, in_, func, bias=0.0, scale=1.0, accum=None, km=None):
        kw = {}
        if accum is not None:
            kw["accum_out"] = accum
        self.op("act", lambda e: e.activation(out=out, in_=in_, func=func, bias=bias, scale=scale, **kw),
                [out, accum], [in_, bias, scale], km)

    def tt(self, eng, out, a, b, op, km=None):
        self.op(eng, lambda e: e.tensor_tensor(out=out, in0=a, in1=b, op=op), [out], [a, b], km)

    def ts(self, eng, out, a, s1, s2=None, op0=ALU.mult, op1=ALU.bypass, accum=None, km=None):
        kw = {}
        if accum is not None:
            kw["accum_out"] = accum
        self.op(eng, lambda e: e.tensor_scalar(out, a, s1, s2, op0, op1, **kw), [out, accum], [a, s1, s2], km)

    def stt(self, eng, out, in0, scalar, in1, op0, op1, km=None):
        self.op(eng, lambda e: e.scalar_tensor_tensor(out=out, in0=in0, scalar=scalar, in1=in1, op0=op0, op1=op1),
                [out], [in0, scalar, in1], km)

    def cp(self, eng, out, in_, km=None):
        if eng == "act":
            self.op(eng, lambda e: e.copy(out, in_), [out], [in_], km)
        else:
            self.op(eng, lambda e: e.tensor_copy(out, in_), [out], [in_], km)

    def memset(self, eng, out, v, km=None):
        self.op(eng, lambda e: e.memset(out, v), [out], [], km)

    def dma(self, q, out, in_, km=None, **kw):
        self.S.dma(q, out, in_, reads=self._k([in_], km), writes=self._k([out], km), **kw)

    def asel(self, out, in_, pattern, base, cm, cmp, fill):
        self.op("pool", lambda e: e.affine_select(out=out, in_=in_, pattern=pattern, base=base,
                                                  channel_multiplier=cm, compare_op=cmp, fill=fill), [out], [in_])


C_GQ, C_GK, C_GV, C_GKF, C_GKB, C_OG = 0, 256, 512, 1024, 1040, 1056
C_SQ, C_SK, C_SV = 1568, 2080, 2208
C_RW = 2336
C_RR, C_RK, C_RV, C_WL, C_AL, C_GL = 2336, 2848, 3360, 3872, 3936, 4000
C_GATE = 4128

W_NAMES = ["w_ada", "b_adaT", "g_mixT", "g_ffnT", "w_in", "w_gk2", "b_gk", "g_gla", "sink", "mu_shift", "w0",
           "w_w2", "a0", "w_a2", "w_g2", "k_k", "k_a", "r_k", "gn_w", "gn_b", "w_branch", "w_out", "w_ffn1",
           "w_ffn3", "w_ffn2", "g_final"]


def build(NZ=2, NX=32, L=2, GRID_W=64, dbg=None):
    NODUMP = False
    if dbg is not None and dbg.startswith("T:"):
        dbg = dbg[2:]
        NODUMP = True
    NT = NZ + NX
    TA = NT * 128
    nc = bass.Bass("TRN2", target_bir_lowering=False)

    def din(name, shape):
        return nc.dram_tensor(name, list(shape), F32, kind="ExternalInput").ap()

    xz = din("xz", [TA, D])
    cT = din("cT", [128, 8, 2])
    w_ada = din("w_ada", [L, D, 6 * D])
    b_adaT = din("b_adaT", [L, 128, 48])
    g_mixT = din("g_mixT", [L, 128, 8])
    g_ffnT = din("g_ffnT", [L, 128, 8])
    w_in = din("w_in", [L, D, INW])
    w_gk2 = din("w_gk2", [L, 2, 16, 256])
    b_gk = din("b_gk", [L, 2, 256])
    g_gla = din("g_gla", [L, 128])
    sink = din("sink", [L, 8])
    mu_shift = din("mu_shift", [L, 1792])
    w0 = din("w0", [L, 2, 512])
    w_w2 = din("w_w2", [L, 2, 64, 512])
    a0 = din("a0", [L, 2, 512])
    w_a2 = din("w_a2", [L, 2, 64, 512])
    w_g2 = din("w_g2", [L, 128, 512])
    k_k = din("k_k", [L, 512])
    k_a = din("k_a", [L, 512])
    r_k = din("r_k", [L, 512])
    gn_w = din("gn_w", [L, 512])
    gn_b = din("gn_b", [L, 512])
    w_branch = din("w_branch", [L, 3, 512, D])
    w_out = din("w_out", [L, D, D])
    w_ffn1 = din("w_ffn1", [L, D, FH])
    w_ffn3 = din("w_ffn3", [L, D, FH])
    w_ffn2 = din("w_ffn2", [L, FH, D])
    g_final = din("g_final", [1, D])
    ropeC = din("ropeC", [NX * 128, 64])
    ropeS = din("ropeS", [NX * 128, 64])
    yout = nc.dram_tensor("yout", [NX * 128, D], F32, kind="ExternalOutput").ap()

    def dscr(name, shape):
        return nc.dram_tensor(name, list(shape), F32, kind="Internal").ap()

    P = dscr("P_scr", [TA, INW])
    XS = dscr("X_scr", [TA, D])
    MODROW = dscr("modrow", [2, 48 * 128])
    YA = dscr("YA_scr", [TA, 512])
    YB = dscr("YB_scr", [TA, 512])
    YC = dscr("YC_scr", [TA, 512])
    OF = dscr("OF_scr", [TA, 512])
    YF = dscr("YF_scr", [TA, 512])
    KF = dscr("KF_scr", [TA, 512])
    RG = dscr("RG_scr", [TA, 512])
    W13B = nc.dram_tensor("W13B_scr", [22, 128, 2048], BF16, kind="Internal").ap()
    dbg_out = None
    if dbg is not None:
        dbg_out = nc.dram_tensor("dbg", [TA, INW], F32, kind="ExternalOutput").ap()

    with contextlib.ExitStack() as st:
        mk = MK(nc, st)
        S = mk.S
        ident = mk.sb("ident", [128, 128])
        mk.memset("pool", ident[:], 1.0)
        mk.asel(ident[:], ident[:], [[-1, 128]], 0, 1, ALU.is_equal, 0.0)
        def mask(name, pat, cm, cmp):
            t = mk.sb(name, [128, 128])
            mk.memset("pool", t[:], 1.0)
            mk.asel(t[:], t[:], [[pat, 128]], 0, cm, cmp, 0.0)
            return t
        M_GE = mask("M_GE", -1, 1, ALU.is_ge)
        M_GT = mask("M_GT", -1, 1, ALU.is_gt)
        M_LE = mask("M_LE", 1, -1, ALU.is_ge)
        M_LT = mask("M_LT", 1, -1, ALU.is_gt)
        SBLK = mk.sb("SBLK", [128, 128])
        mk.memset("pool", SBLK[:], 0.0)
        mk.memset("pool", SBLK[0:64, 0:64], 1.0)
        mk.memset("pool", SBLK[64:128, 64:128], 1.0)
        BM = {}
        for nm, m_ in (("GE", M_GE), ("GT", M_GT), ("LE", M_LE), ("LT", M_LT)):
            BM[nm] = mk.sb("B" + nm, [128, 128])
            mk.tt("dve", BM[nm][:], m_[:], SBLK[:], ALU.mult)
        ones = mk.sb("ones", [128, 2])
        mk.memset("pool", ones[:], 1.0)
        cind = mk.sb("cind", [128, 2])
        mk.memset("pool", cind[:], 0.0)
        mk.memset("pool", cind[0:64, 0:1], 1.0)
        mk.memset("pool", cind[64:128, 1:2], 1.0)
        identb = mk.sb("identb", [128, 128], BF16)
        mk.cp("dve", identb[:], ident[:])
        psA = Ring(mk, "psA", 6, [128, 512], F32, psum=True)
        psB = Ring(mk, "psB", 2, [128, 1024], BF16, psum=True)
        scT = mk.sb("scT", [128, 8, 2])
        mk.dma("sp", scT[:], cT[:, :, :])
        mk.act(scT[:], scT[:], AF.Silu)
        modsb = mk.sb("modsb", [128, 48, 2])
        badat = mk.sb("badat", [128, 48])
        gmix = mk.sb("gmix", [128, 8])
        gffn = mk.sb("gffn", [128, 8])
        A1 = mk.sb("A1", [128, 8, 2])
        A2 = mk.sb("A2", [128, 8, 2])

        def hkey(ti):
            return {hT[:].name: [("hT", ti)]}

        def norm_to_hT(src_ap_fn, Acoef, Bcoef_j0, skip_z):
            for ti in range(NT):
                if skip_z and ti < NZ:
                    continue
                w = 1 if ti < NZ else 0
                xt = xin.get()
                mk.dma("sp", xt[:], src_ap_fn(ti), km={src_ap_fn(ti).name: [(src_ap_fn(ti).name, ti)]})
                ss = st1.get()
                mk.act(sq[:], xt[:], AF.Square, accum=ss[:])
                mk.act(ss[:], ss[:], AF.Ln, bias=1e-6, scale=1.0 / D)
                mk.act(ss[:], ss[:], AF.Exp, scale=-0.5)
                dgt = dg.get()
                mk.ts("dve", dgt[:], ident[:], ss[:, 0:1])
                for half in range(2):
                    ps = psA.get()
                    for kk in range(4):
                        k = half * 4 + kk
                        mk.mm(ps[:, kk * 128:(kk + 1) * 128], xt[:, k * 128:(k + 1) * 128], dgt[:])
                    for kk in range(4):
                        k = half * 4 + kk
                        mk.act(hT[:, k, ti * 128:(ti + 1) * 128], ps[:, kk * 128:(kk + 1) * 128], AF.Identity,
                               bias=modsb[:, Bcoef_j0 + k, w:w + 1], scale=Acoef[:, k, w:w + 1], km=hkey(ti))

        for l in range(L):
            need_ctx = l < L - 1
            mk.push()
            hT = mk.sb("hT", [128, 8, TA], BF16)
            wst = Ring(mk, "wst", 2, [128, 8, 512], F32)
            wbf = Ring(mk, "wbf", 2, [128, 8, 512], BF16)
            xin = Ring(mk, "xin", 2, [128, D], F32)
            sq = mk.sb("sqjunk", [128, D])
            st1 = Ring(mk, "st1", 4, [128, 1], F32)
            dg = Ring(mk, "dg", 2, [128, 128], F32)
            ev = Ring(mk, "ev", 6, [128, 512], F32)
            hs = Ring(mk, "hs", 2, [128, 8, 128], BF16)
            mu_om = mk.sb("mu_om", [128, 1792])
            mu_h = mk.sb("mu_h", [128, 1792])
            mk.dma("sp", badat[:], b_adaT[l])
            mk.dma("sp", gmix[:], g_mixT[l])
            mk.dma("sp", gffn[:], g_ffnT[l])
            wv = w_ada[l].rearrange("(k p) n -> p k n", p=128)
            for g in range(12):
                wt = wst.get()
                mk.dma("sp", wt[:], wv[:, :, g * 512:(g + 1) * 512])
                ps = psA.get()
                for j in range(4):
                    for k in range(8):
                        mk.mm(ps[:, 2 * j:2 * j + 2], wt[:, k, j * 128:(j + 1) * 128], scT[:, k, :],
                              start=(k == 0), stop=(k == 7))
                for w in range(2):
                    mk.tt("dve", modsb[:, g * 4:(g + 1) * 4, w], ps[:, 0:8].rearrange("p (j w) -> p j w", w=2)[:, :, w],
                          badat[:, g * 4:(g + 1) * 4], ALU.add)
            for w in range(2):
                mk.stt("dve", A1[:, :, w], modsb[:, 8:16, w], 1.0, gmix[:], ALU.add, ALU.mult)
                mk.stt("dve", A2[:, :, w], modsb[:, 32:40, w], 1.0, gffn[:], ALU.add, ALU.mult)
                mk.dma("pool", MODROW[w].rearrange("(j p) -> p j", p=128), modsb[:, :, w],
                       allow_slow_non_contiguous=True)
            src = (lambda ti: xz[ti * 128:(ti + 1) * 128, :]) if l == 0 else (lambda ti: XS[ti * 128:(ti + 1) * 128, :])
            norm_to_hT(src, A1, 0, False)
            mk.dma("sp", mu_om[:], mu_shift[l:l + 1, :].broadcast_to([128, 1792]))
            mk.ts("dve", mu_h[:], mu_om[:], 0.5)
            mk.ts("dve", mu_om[:], mu_om[:], -1.0, 1.0, ALU.mult, ALU.add)
            groups = []
            for (a, b, rw) in ((0, C_RW, False), (C_RW, C_GATE, True), (C_GATE, INW, False)):
                c = a
                while c < b:
                    groups.append((c, min(c + 512, b), rw))
                    c += 512
            wiv = w_in[l].rearrange("(k p) n -> p k n", p=128)
            def load_group(gidx):
                c0_, c1_, _rw = groups[gidx]
                wd_ = c1_ - c0_
                wt_ = wst.get()
                mk.dma("sp", wt_[:, :, 0:wd_], wiv[:, :, c0_:c1_])
                wb_n = wbf.get()
                mk.cp("pool", wb_n[:, :, 0:wd_], wt_[:, :, 0:wd_])
                return wb_n
            wb_next = load_group(0)
            for gidx_, (c0, c1, rw) in enumerate(groups):
                wd = c1 - c0
                wb = wb_next
                if gidx_ + 1 < len(groups):
                    wb_next = load_group(gidx_ + 1)
                for ti in range(NT):
                    if c0 >= C_GATE and ti < NZ and not need_ctx:
                        continue
                    ps = psA.get()
                    for k in range(8):
                        mk.mm(ps[:, 0:wd], hT[:, k, ti * 128:(ti + 1) * 128], wb[:, k, 0:wd], start=(k == 0),
                              stop=(k == 7), km=hkey(ti))
                    e1 = ev.get()
                    if not rw:
                        mk.cp("act", e1[:, 0:wd], ps[:, 0:wd])
                    else:
                        t0 = ti * 128
                        first = ti in (0, NZ)
                        last = ti in (NZ - 1, NT - 1)
                        h2 = hs.get()
                        lo = 1 if first else 0
                        hi = 127 if last else 128
                        km3 = {hT[:].name: [("hT", x) for x in (ti - 1, ti, ti + 1) if 0 <= x < NT]}
                        mk.tt("dve", h2[:, :, lo:hi], hT[:, :, t0 + lo - 1:t0 + hi - 1], hT[:, :, t0 + lo + 1:t0 + hi + 1],
                              ALU.add, km=km3)
                        if first:
                            mk.cp("dve", h2[:, :, 0:1], hT[:, :, t0 + 1:t0 + 2], km=km3)
                        if last:
                            mk.cp("dve", h2[:, :, 127:128], hT[:, :, t0 + 126:t0 + 127], km=km3)
                        ps2 = psA.get()
                        for k in range(8):
                            mk.mm(ps2[:, 0:wd], h2[:, k, :], wb[:, k, 0:wd], start=(k == 0), stop=(k == 7))
                        m0 = c0 - C_RW
                        e2 = ev.get()
                        mk.tt("dve", e1[:, 0:wd], ps[:, 0:wd], mu_om[:, m0:m0 + wd], ALU.mult)
                        mk.tt("dve", e2[:, 0:wd], ps2[:, 0:wd], mu_h[:, m0:m0 + wd], ALU.mult)
                        mk.tt("pool", e1[:, 0:wd], e1[:, 0:wd], e2[:, 0:wd], ALU.add)
                    mk.dma("pool", P[ti * 128:(ti + 1) * 128, c0:c1], e1[:, 0:wd], km={"P_scr": [("P", ti)]})
            mk.pop()
            if dbg == "P%d" % l:
                break
            mk.push()
            ev = Ring(mk, "ev", 6, [128, 512], F32)
            if True:
                W2a = [mk.sb("W2a%d" % d_, [33, 256]) for d_ in range(2)]
                gglab = mk.sb("gglab", [128, 128])
                Sst = [[mk.sb("Sst%d%d" % (d_, p_), [128, 256]) for p_ in range(2)] for d_ in range(2)]
                gin = Ring(mk, "gin", 2, [128, 1568], F32)
                gkTa = Ring(mk, "gkTa", 2, [33, 128], F32)
                for t_ in gkTa.t:
                    mk.memset("pool", t_[:], 1.0)
                g256 = Ring(mk, "g256", 8, [128, 256], F32)
                qkT = Ring(mk, "qkT", 4, [128, 256], F32R)
                khR = Ring(mk, "khR", 2, [128, 256], F32R)
                gvr = Ring(mk, "gvr", 2, [128, 512], F32R)
                Sst_r = [[mk.sb("Sstr%d%d" % (d_, p_), [128, 256], F32R) for p_ in range(2)] for d_ in range(2)]
                attS = Ring(mk, "attS", 3, [128, 128], F32R)
                decs = Ring(mk, "decs", 2, [128, 2], F32)
                st4 = Ring(mk, "st4", 4, [128, 4], F32)
            for d_ in range(2):
                mk.memset("pool", W2a[d_][:], 0.0)
                mk.dma("sp", W2a[d_][16 * d_:16 * d_ + 16, :], w_gk2[l, d_])
                mk.dma("sp", W2a[d_][32:33, :], b_gk[l, d_:d_ + 1, :])
                for p_ in range(2):
                    mk.memset("pool", Sst[d_][p_][:], 0.0)
                    mk.cp("pool", Sst_r[d_][p_][:], Sst[d_][p_][:])
            mk.dma("sp", gglab[:], g_gla[l:l + 1, :].broadcast_to([128, 128]))
            evs = Ring(mk, "ev_s", 3, [128, 512], F32)

            class RV:
                def __init__(self, ts):
                    self.t, self.i = ts, 0

                def get(self):
                    t_ = self.t[self.i]
                    self.i = (self.i + 1) % len(self.t)
                    return t_
            psGt, psGo = RV([psA.t[0], psA.t[1]]), psA.t[2]
            psSt, psSo = RV([psA.t[3], psA.t[4]]), psA.t[5]
            if True:
                KTd = [mk.sb("KTd%d" % i_, [128, TA], BF16) for i_ in range(2)]
                VX = mk.sb("VX", [128, NT, 128], BF16)
                sin_ = Ring(mk, "sin_", 2, [128, 768], F32)
                rope_t = Ring(mk, "rope_t", 2, [128, 128], F32)
                rq = Ring(mk, "rq", 4, [128, 640], F32)
                kd = Ring(mk, "kd", 2, [128, 256], F32)
                qTr = Ring(mk, "qTr", 3, [128, 4, 128], BF16)
                sinkb = mk.sb("sinkb", [128, 8])
                nsinkb = mk.sb("nsinkb", [128, 8])
                maskL = mk.sb("maskL", [128, 384])
                mk.memset("pool", maskL[:], 0.0)
                mk.asel(maskL[:, 0:128], maskL[:, 0:128], [[1, 128]], 0, -1, ALU.is_ge, -1e30)
                mk.asel(maskL[:, 256:384], maskL[:, 256:384], [[-1, 128]], 0, 1, ALU.is_ge, -1e30)
                scS = Ring(mk, "scS", 2, [128, 640], F32)
                pbS = Ring(mk, "pbS", 2, [128, 640], BF16)
                pTS = Ring(mk, "pTS", 2, [128, 640], BF16)
                st1b = Ring(mk, "st1b", 12, [128, 1], F32)
            mk.dma("sp", sinkb[:], sink[l:l + 1, :].broadcast_to([128, 8]))
            mk.ts("dve", nsinkb[:], sinkb[:], -1.0)

            def swa_queries(ti, qT, loc):
                yb = evs.get()
                psO = psSo
                nctx = NZ
                for h in range(8):
                    kv, c_, b0 = h // 4, h // 2, (h % 2) * 64
                    sc = scS.get()
                    wl = 0
                    if loc is not None:
                        kt0, nk, mo = loc
                        wl = nk * 128
                        ps1 = psSt.get()
                        mk.mm(ps1[:, 0:wl], qT[b0:b0 + 64, c_, :], KTd[kv][b0:b0 + 64, kt0 * 128:kt0 * 128 + wl])
                        mk.tt("dve", sc[:, 0:wl], ps1[:, 0:wl], maskL[:, mo:mo + wl], ALU.add)
                    ps2 = psSt.get()
                    mk.mm(ps2[:, 0:nctx * 128], qT[b0:b0 + 64, c_, :], KTd[kv][b0:b0 + 64, 0:nctx * 128])
                    wt = wl + nctx * 128
                    mk.cp("act", sc[:, wl:wt], ps2[:, 0:nctx * 128])
                    yield
                    mx = st1b.get()
                    mk.op("dve", lambda e, mx=mx, sc=sc, wt=wt: e.reduce_max(out=mx[:], in_=sc[:, 0:wt], axis=AX.X), [mx[:]], [sc[:]])
                    mk.ts("dve", mx[:], mx[:], -0.125, nsinkb[:, h:h + 1], ALU.mult, ALU.min)
                    yield
                    pb = pbS.get()
                    sm = st1b.get()
                    mk.act(pb[:, 0:wt], sc[:, 0:wt], AF.Exp, bias=mx[:, 0:1], scale=0.125, accum=sm[:])
                    es = st1b.get()
                    mk.act(es[:], sinkb[:, h:h + 1], AF.Exp, bias=mx[:, 0:1])
                    mk.tt("dve", sm[:], sm[:], es[:], ALU.add)
                    mk.op("dve", lambda e, sm=sm: e.reciprocal(out=sm[:], in_=sm[:]), [sm[:]], [sm[:]])
                    yield
                    nch = wt // 128
                    psT = psB.get()
                    for c2 in range(nch):
                        mk.tr(psT[:, c2 * 128:(c2 + 1) * 128], pb[:, c2 * 128:(c2 + 1) * 128], identb[:])
                    pT = pTS.get()
                    mk.cp("act", pT[:, 0:wt], psT[:, 0:wt])
                    yield
                    for c2 in range(nch):
                        if loc is not None and c2 < loc[1]:
                            kt = loc[0] + c2
                        else:
                            kt = c2 - (loc[1] if loc is not None else 0)
                        mk.mm(psO[:, h * 64:(h + 1) * 64], pT[:, c2 * 128:(c2 + 1) * 128], VX[:, kt, kv * 64:(kv + 1) * 64],
                              start=(c2 == 0), stop=(c2 == nch - 1))
                    mk.ts("dve", yb[:, h * 64:(h + 1) * 64], psO[:, h * 64:(h + 1) * 64], sm[:, 0:1])
                    yield
                mk.dma("pool", YB[ti * 128:(ti + 1) * 128, :], yb[:], km={"YB_scr": [("YB_scr", ti)]})

            def gla_gen():
                for d_ in range(2):
                    order = list(range(NT)) if d_ == 0 else (list(range(NZ - 1, -1, -1)) + list(range(NT - 1, NZ - 1, -1)))
                    Mc, Mr = (M_LE, M_GT) if d_ == 0 else (M_GE, M_LT)
                    for ti in order:
                        want_o = need_ctx or ti >= NZ
                        r0_, r1_ = ti * 128, (ti + 1) * 128
                        gi = gin.get()
                        mk.dma("sp", gi[:], P[r0_:r1_, 0:1568], km={"P_scr": [("P", ti)]})
                        ps = psGt.get()
                        mk.tr(ps[0:32, 0:128], gi[:, C_GKF:C_GKF + 32], ident[:])
                        gk = gkTa.get()
                        mk.cp("act", gk[0:32, :], ps[0:32, 0:128])
                        yield
                        ps = psGt.get()
                        mk.mm(ps[:, 0:256], gk[0:33, :], W2a[d_][0:33, :])
                        lg = g256.get()
                        mk.act(lg[:], ps[:, 0:256], AF.Exp, scale=-1.0)
                        mk.act(lg[:], lg[:], AF.Ln, bias=1.0)
                        mk.ts("dve", lg[:], lg[:], -1.0 / 16.0)
                        yield
                        psC = psGt.get()
                        mk.mm(psC[:, 0:256], Mc[:], lg[:])
                        mk.mm(psC[:, 256:512], Mr[:], lg[:])
                        psD = psGt.get()
                        for p_ in range(2):
                            mk.mm(psD[:, p_:p_ + 1], lg[:, p_ * 128:(p_ + 1) * 128], ones[:, 0:1])
                        dc = decs.get()
                        mk.act(dc[:], psD[:, 0:2], AF.Exp)
                        yield
                        kh0 = g256.get()
                        mk.act(kh0[:], psC[:, 256:512], AF.Exp)
                        khat = khR.get()
                        mk.tt("dve", khat[:], kh0[:], gi[:, C_GK:C_GK + 256], ALU.mult)
                        gv = gvr.get()
                        mk.cp("pool", gv[:], gi[:, C_GV:C_GV + 512])
                        yield
                        if want_o:
                            qt = g256.get()
                            mk.act(qt[:], psC[:, 0:256], AF.Exp)
                            mk.stt("dve", qt[:], qt[:], 0.125, gi[:, C_GQ:C_GQ + 256], ALU.mult, ALU.mult)
                            kt = g256.get()
                            mk.act(kt[:], psC[:, 0:256], AF.Exp, scale=-1.0)
                            mk.tt("dve", kt[:], kt[:], gi[:, C_GK:C_GK + 256], ALU.mult)
                            yield
                            qk = []
                            for p_ in range(2):
                                psT = psGt.get()
                                mk.tr(psT[:, 0:128], qt[:, p_ * 128:(p_ + 1) * 128], ident[:])
                                mk.tr(psT[:, 128:256], kt[:, p_ * 128:(p_ + 1) * 128], ident[:])
                                q_ = qkT.get()
                                mk.cp("act", q_[:], psT[:, 0:256])
                                qk.append(q_)
                                yield
                            psO = psGo
                            for h in range(4):
                                p_, s_ = h // 2, h % 2
                                b0 = s_ * 64
                                psAt = psGt.get()
                                mk.mm(psAt[:, 0:128], qk[p_][b0:b0 + 64, 128:256], qk[p_][b0:b0 + 64, 0:128])
                                at = attS.get()
                                mk.tt("dve", at[:], psAt[:, 0:128], (M_LE if d_ == 0 else M_GE)[:], ALU.mult)
                                yield
                                mk.mm(psO[:, h * 128:(h + 1) * 128], at[:], gv[:, h * 128:(h + 1) * 128],
                                      start=True, stop=False)
                                mk.mm(psO[:, h * 128:(h + 1) * 128], qk[p_][b0:b0 + 64, 0:128],
                                      Sst_r[d_][p_][b0:b0 + 64, s_ * 128:(s_ + 1) * 128], start=False, stop=True)
                                yield
                            ot = ev.get()
                            if d_ == 0:
                                mk.cp("act", ot[:], psO[:])
                                mk.dma("pool", OF[r0_:r1_, :], ot[:], km={"OF_scr": [("OF", ti)]})
                            else:
                                of = ev.get()
                                mk.dma("sp", of[:], OF[r0_:r1_, :], km={"OF_scr": [("OF", ti)]})
                                mk.tt("dve", ot[:], psO[:], of[:], ALU.add)
                                yield
                                mk.tt("dve", of[:], ot[:], ot[:], ALU.mult)
                                s4 = st4.get()
                                mk.op("dve", lambda e, s4=s4, of=of: e.reduce_sum(out=s4[:], in_=of[:].rearrange("p (h v) -> p h v", h=4), axis=AX.X), [s4[:]], [of[:]])
                                mk.act(s4[:], s4[:], AF.Ln, bias=1e-6, scale=1.0 / 128)
                                mk.act(s4[:], s4[:], AF.Exp, scale=-0.5)
                                yield
                                for h in range(4):
                                    mk.stt("dve", ot[:, h * 128:(h + 1) * 128], ot[:, h * 128:(h + 1) * 128], s4[:, h:h + 1],
                                           gglab[:], ALU.mult, ALU.mult)
                                mk.act(of[:], gi[:, C_OG:C_OG + 512], AF.Silu)
                                mk.tt("dve", ot[:], ot[:], of[:], ALU.mult)
                                mk.dma("pool", YA[r0_:r1_, :], ot[:], km={"YA_scr": [("YA_scr", ti)]})
                        for p_ in range(2):
                            psS = psGt.get()
                            mk.mm(psS[:, 0:256], khat[:, p_ * 128:(p_ + 1) * 128], gv[:, p_ * 256:(p_ + 1) * 256])
                            mk.stt("dve", Sst[d_][p_][:], Sst[d_][p_][:], dc[:, p_:p_ + 1], psS[:, 0:256], ALU.mult, ALU.add)
                            mk.cp("act", Sst_r[d_][p_][:], Sst[d_][p_][:])
                            yield
                        yield
            def swa_gen():
                qTs = {}
                for ti in range(NT + 1):
                    if ti < NT:
                        r0_, r1_ = ti * 128, (ti + 1) * 128
                        si = sin_.get()
                        mk.dma("sp", si[:], P[r0_:r1_, C_SQ:C_SQ + 768], km={"P_scr": [("P", ti)]})
                        if ti >= NZ:
                            tx = (ti - NZ) * 128
                            rt = rope_t.get()
                            mk.dma("sp", rt[:, 0:64], ropeC[tx:tx + 128, :])
                            mk.dma("sp", rt[:, 64:128], ropeS[tx:tx + 128, :])
                            xv = si[:, 0:640].rearrange("p (h d) -> p h d", d=64)
                            t1 = rq.get()
                            t2 = rq.get()
                            mk.tt("dve", t1[:].rearrange("p (h d) -> p h d", d=64), xv,
                                  rt[:, 0:64].unsqueeze(1).broadcast_to([128, 10, 64]), ALU.mult)
                            x5 = si[:, 0:640].rearrange("p (h a s f) -> p h a s f", a=2, s=2, f=16)
                            o5 = t2[:].rearrange("p (h a s f) -> p h a s f", a=2, s=2, f=16)
                            s5 = rt[:, 64:128].rearrange("p (a s f) -> p a s f", a=2, s=2).unsqueeze(1).broadcast_to([128, 10, 2, 2, 16])
                            for s_ in range(2):
                                mk.tt("dve", o5[:, :, :, s_, :], x5[:, :, :, 1 - s_, :], s5[:, :, :, s_, :], ALU.mult)
                            mk.tt("pool", t1[:], t1[:], t2[:], ALU.add)
                            yield
                            qk_ap = t1
                        else:
                            qk_ap = si
                        qT = qTr.get()
                        psT = psSt.get()
                        for c_ in range(4):
                            mk.tr(psT[:, c_ * 128:(c_ + 1) * 128], qk_ap[:, c_ * 128:(c_ + 1) * 128], ident[:])
                        mk.cp("act", qT[:].rearrange("p c t -> p (c t)"), psT[:, 0:512])
                        yield
                        kdt = kd.get()
                        for kv in range(2):
                            for dup in range(2):
                                mk.cp("pool", kdt[:, kv * 128 + dup * 64:kv * 128 + dup * 64 + 64],
                                      qk_ap[:, 512 + kv * 64:512 + kv * 64 + 64])
                        psK = psSt.get()
                        for kv in range(2):
                            mk.tr(psK[:, kv * 128:(kv + 1) * 128], kdt[:, kv * 128:(kv + 1) * 128], ident[:])
                            mk.cp("act", KTd[kv][:, r0_:r1_], psK[:, kv * 128:(kv + 1) * 128], km={KTd[kv][:].name: [("KT", kv, ti)]})
                        mk.cp("dve", VX[:, ti, :], si[:, 640:768], km={VX[:].name: [("VX", ti)]})
                        qTs[ti] = qT
                    todo = []
                    if ti == NZ - 1 and need_ctx:
                        todo = [(z_, None) for z_ in range(NZ)]
                    if ti - 1 >= NZ:
                        tq = ti - 1
                        a_ = max(tq - 1, NZ)
                        b_ = min(tq + 1, NT - 1)
                        todo = [(tq, (a_, b_ - a_ + 1, 128 * (a_ - (tq - 1))))]
                    for (tq, loc) in todo:
                        kmq = {}
                        for kv in range(2):
                            ks = list(range(NZ)) + ([] if loc is None else list(range(loc[0], loc[0] + loc[1])))
                            kmq[KTd[kv][:].name] = [("KT", kv, x_) for x_ in ks]
                        kmq[VX[:].name] = [("VX", x_) for x_ in (list(range(NZ)) + ([] if loc is None else list(range(loc[0], loc[0] + loc[1]))))]
                        _op = mk.op

                        def op2(eng, fn, outs, ins, km=None, kmq=kmq):
                            _op(eng, fn, outs, ins, kmq if km is None else km)
                        mk.op = op2
                        yield from swa_queries(tq, qTs[tq], loc)
                        mk.op = _op
                    yield
            gen_a, gen_b = gla_gen(), swa_gen()
            alive = [True, True]
            while any(alive):
                for gi_, (g_, reps) in enumerate(((gen_a, 1), (gen_b, 1))):
                    for _ in range(reps):
                        if alive[gi_]:
                            try:
                                next(g_)
                            except StopIteration:
                                alive[gi_] = False
            mk.pop()
            if dbg == "SWA%d" % l:
                break
            mk.push()
            LAM = 0.6065306597126334
            bcs = {}
            for nm_, src_ in (("kk", k_k[l:l + 1, :]), ("ka", k_a[l:l + 1, :]), ("rk", r_k[l:l + 1, :]),
                              ("gw", gn_w[l:l + 1, :]), ("gb", gn_b[l:l + 1, :]),
                              ("w00", w0[l, 0:1, :]), ("w01", w0[l, 1:2, :]), ("a00", a0[l, 0:1, :]), ("a01", a0[l, 1:2, :])):
                bcs[nm_] = mk.sb("bc_" + nm_, [128, 512])
                mk.dma("sp", bcs[nm_][:], src_.broadcast_to([128, 512]))
            ww2 = [mk.sb("ww2%d" % d_, [64, 512]) for d_ in range(2)]
            wa2 = [mk.sb("wa2%d" % d_, [128, 512]) for d_ in range(2)]
            wg2 = mk.sb("wg2", [128, 512])
            mk.dma("sp", wg2[:], w_g2[l])
            Sta = [mk.sb("Sta%d" % d_, [128, 512]) for d_ in range(2)]
            Sta_r = [mk.sb("Star%d" % d_, [128, 512], F32R) for d_ in range(2)]
            for d_ in range(2):
                mk.dma("sp", ww2[d_][:], w_w2[l, d_])
                mk.dma("sp", wa2[d_][64:128, :], w_a2[l, d_])
                mk.memset("pool", Sta[d_][:], 0.0)
                mk.cp("pool", Sta_r[d_][:], Sta[d_][:])
            rin = Ring(mk, "rin", 2, [128, 1792], F32)
            lrT = Ring(mk, "lrT", 4, [128, 128], F32)
            r5 = Ring(mk, "r5", 21, [128, 512], F32)
            FTr = Ring(mk, "FTr", 6, [128, 512], F32R)
            AXr = Ring(mk, "AXr", 9, [128, 512], F32R)
            wallR = Ring(mk, "wallR", 3, [128, 8, 384], F32R)
            rhallR = Ring(mk, "rhallR", 2, [128, 8, 128], F32R)
            bkR = Ring(mk, "bkR", 4, [128, 512], F32R)
            vrR = Ring(mk, "vrR", 2, [128, 512], F32R)
            aptR = Ring(mk, "aptR", 1, [128, 512], F32R)
            psbR = Ring(mk, "psbR", 2, [128, 512], F32R)

            s8r = Ring(mk, "s8r", 6, [128, 8], F32)
            maskX = []
            maskM = []
            for d_ in range(2):
                mx_ = mk.sb("maskX%d" % d_, [128, 512])
                strict, incl, mm_ = (BM["LT"], BM["LE"], BM["GT"]) if d_ == 0 else (BM["GT"], BM["GE"], BM["LT"])
                for i_, m_ in enumerate((strict, incl, strict, incl)):
                    mk.cp("pool", mx_[:, i_ * 128:(i_ + 1) * 128], m_[:])
                maskX.append(mx_)
                maskM.append(mm_)
            for d_ in range(2):
                order = list(range(NT)) if d_ == 0 else (list(range(NZ - 1, -1, -1)) + list(range(NT - 1, NZ - 1, -1)))
                Bincl, Bexcl, Brev = (BM["LE"], BM["LT"], BM["GT"]) if d_ == 0 else (BM["GE"], BM["GT"], BM["LT"])
                corder = (0, 1) if d_ == 0 else (1, 0)
                for ti in order:
                    want_o = need_ctx or ti >= NZ
                    r0_, r1_ = ti * 128, (ti + 1) * 128
                    ri = rin.get()
                    mk.dma("sp", ri[:], P[r0_:r1_, C_RR:C_GATE], km={"P_scr": [("P", ti)]})
                    r_, k_, v_ = ri[:, 0:512], ri[:, 512:1024], ri[:, 1024:1536]
                    psT = psA.get()
                    mk.tr(psT[:, 0:128], ri[:, 1536:1664], ident[:])
                    mk.tr(psT[:, 128:256], ri[:, 1664:1792], ident[:])
                    la = lrT.get()
                    mk.act(la[0:64, :], psT[0:64, 0:128], AF.Tanh)
                    mk.cp("dve", la[64:128, :], psT[64:128, 0:128])
                    kk = r5.get()
                    mk.tt("dve", kk[:], k_, bcs["kk"][:], ALU.mult)
                    sqv = r5.get()
                    mk.act(sqv[:], kk[:], AF.Square)
                    s8 = s8r.get()
                    mk.op("dve", lambda e, s8=s8, sqv=sqv: e.reduce_sum(out=s8[:], in_=sqv[:].rearrange("p (h k) -> p h k", h=8), axis=AX.X), [s8[:]], [sqv[:]])
                    mk.ts("dve", s8[:], s8[:], 1e-24, None, ALU.max)
                    mk.act(s8[:], s8[:], AF.Ln)
                    mk.act(s8[:], s8[:], AF.Exp, scale=-0.5)
                    mk.tt("dve", kk[:].rearrange("p (h k) -> p h k", h=8), kk[:].rearrange("p (h k) -> p h k", h=8),
                          s8[:, :].unsqueeze(2).broadcast_to([128, 8, 64]), ALU.mult)
                    if d_ == 0 and want_o:
                        lb = lrT.get()
                        mk.act(lb[:], psT[:, 128:256], AF.Sigmoid)
                        psG = psA.get()
                        mk.mm(psG[:], lb[:], wg2[:])
                        gg = r5.get()
                        mk.cp("act", gg[:], psG[:])
                        mk.dma("pool", RG[r0_:r1_, :], gg[:], km={"RG_scr": [("RG", ti)]})
                    psU = psA.get()
                    mk.mm(psU[:], la[0:64, :], ww2[d_][0:64, :])
                    sg = r5.get()
                    mk.tt("dve", sg[:], psU[:], bcs["w0%d" % d_][:], ALU.add)
                    mk.act(sg[:], sg[:], AF.Sigmoid)
                    psU = psA.get()
                    mk.mm(psU[:], la[64:128, :], wa2[d_][64:128, :])
                    aa = r5.get()
                    mk.tt("dve", aa[:], psU[:], bcs["a0%d" % d_][:], ALU.add)
                    mk.act(aa[:], aa[:], AF.Sigmoid)
                    km_ = r5.get()
                    mk.stt("dve", km_[:], aa[:], -1.0, bcs["ka"][:], ALU.add, ALU.mult)
                    mk.stt("dve", km_[:], km_[:], 1.0, k_, ALU.add, ALU.mult)
                    bp = r5.get()
                    mk.tt("dve", bp[:], kk[:], aa[:], ALU.mult)
                    if d_ == 0:
                        if want_o:
                            mk.dma("pool", KF[r0_:r1_, :], km_[:], km={"KF_scr": [("KF", ti)]})
                    psC = psA.get()
                    mk.mm(psC[:], Bincl[:], sg[:])
                    Ei = r5.get()
                    mk.act(Ei[:], psC[:], AF.Exp, scale=-LAM)
                    Em = r5.get()
                    mk.act(Em[:], psC[:], AF.Exp, scale=LAM)
                    psC = psA.get()
                    mk.mm(psC[:], Bexcl[:], sg[:])
                    Ee = r5.get()
                    mk.act(Ee[:], psC[:], AF.Exp, scale=-LAM)
                    psC = psA.get()
                    mk.mm(psC[:], Brev[:], sg[:])
                    Er = r5.get()
                    mk.act(Er[:], psC[:], AF.Exp, scale=-LAM)
                    psD = psA.get()
                    for p_ in range(4):
                        mk.mm(psD[:, 2 * p_:2 * p_ + 2], sg[:, p_ * 128:(p_ + 1) * 128], cind[:, 0:2])
                    dcs = s8r.get()
                    mk.act(dcs[:], psD[:, 0:8], AF.Exp, scale=-LAM)
                    At = Ee
                    mk.stt("dve", At[:], kk[:], -1.0, Ee[:], ALU.mult, ALU.mult)
                    Rt = Ei
                    mk.tt("pool", Rt[:], r_, Ei[:], ALU.mult)
                    Bt = r5.get()
                    mk.tt("dve", Bt[:], bp[:], Em[:], ALU.mult)
                    Kt = Em
                    mk.tt("dve", Kt[:], km_[:], Em[:], ALU.mult)
                    Bh = bkR.get()
                    mk.tt("pool", Bh[:], bp[:], Er[:], ALU.mult)
                    Kh = bkR.get()
                    mk.tt("dve", Kh[:], km_[:], Er[:], ALU.mult)
                    v_r = vrR.get()
                    mk.cp("act", v_r[:], v_)
                    yt = r5.get()
                    FTs = []
                    for p_ in range(4):
                        pc = slice(p_ * 128, (p_ + 1) * 128)
                        psF = psA.get()
                        for i_, src_ in enumerate((At, Rt, Bt, Kt)):
                            mk.tr(psF[:, i_ * 128:(i_ + 1) * 128], src_[:, pc], ident[:])
                        FT = FTr.get()
                        mk.cp("act", FT[:], psF[:])
                        FTs.append(FT)
                    AXs = []
                    allk = lambda T_: {T_[:].name: [(T_[:].name, h_) for h_ in range(8)]}
                    Wall = wallR.get()
                    for h in range(8):
                        p_, b0 = h // 2, (h % 2) * 64
                        FT = FTs[p_]
                        kh_ = {Wall[:].name: [(Wall[:].name, h)]}
                        psX = psA.get()
                        mk.mm(psX[:, 0:256], FT[b0:b0 + 64, 256:384], FT[b0:b0 + 64, 0:256])
                        mk.mm(psX[:, 256:512], FT[b0:b0 + 64, 384:512], FT[b0:b0 + 64, 0:256])
                        AX_ = AXr.get()
                        mk.tt("dve", AX_[:], psX[:], maskX[d_][:], ALU.mult)
                        AXs.append(AX_)
                        psY = psA.get()
                        mk.mm(psY[:, 0:128], FT[b0:b0 + 64, 0:128], FT[b0:b0 + 64, 256:384])
                        mk.tt("dve", Wall[:, h, 128:256], psY[:, 0:128], maskM[d_][:], ALU.mult, km=kh_)
                        mk.cp("act", Wall[:, h, 256:384], AX_[:, 0:128].bitcast(F32), km=kh_)
                    psV = psA.get()
                    for h in range(8):
                        mk.mm(psV[:, h * 64:(h + 1) * 64], AXs[h][:, 256:384], v_r[:, h * 64:(h + 1) * 64])
                    mk.cp("pool", Wall[:, :, 0:64], At[:].rearrange("p (h k) -> p h k", h=8), km=allk(Wall))
                    mk.cp("act", Wall[:, :, 64:128], psV[:].rearrange("p (h k) -> p h k", h=8), km=allk(Wall))
                    for j in range(6):
                        lastj = (j == 5)
                        Wn = rhallR.get() if lastj else wallR.get()
                        for h in range(8):
                            kmh = {Wall[:].name: [(Wall[:].name, h)], Wn[:].name: [(Wn[:].name, h)]}
                            psR = psA.get()
                            if not lastj:
                                mk.mm(psR[:, 0:256], Wall[:, h, 256:384], Wall[:, h, 0:256], km=kmh)
                                mk.mm(psR[:, 256:384], Wall[:, h, 128:256], Wall[:, h, 256:384], km=kmh)
                                mk.tt("dve", Wn[:, h, 0:128], psR[:, 0:128], Wall[:, h, 0:128].bitcast(F32), ALU.add, km=kmh)
                                mk.cp("act", Wn[:, h, 128:384], psR[:, 128:384], km=kmh)
                            else:
                                mk.mm(psR[:, 0:128], Wall[:, h, 256:384], Wall[:, h, 0:128], km=kmh)
                                mk.tt("dve", Wn[:, h, :], psR[:, 0:128], Wall[:, h, 0:128].bitcast(F32), ALU.add, km=kmh)
                        Wall = Wn
                    RHall = Wall
                    ApAll = r5.get()
                    mk.cp("dve", ApAll[:].rearrange("p (h k) -> p h k", h=8), RHall[:, :, 0:64].bitcast(F32), km=allk(RHall))
                    psZ = psA.get()
                    for p_ in range(4):
                        mk.tr(psZ[:, p_ * 128:(p_ + 1) * 128], ApAll[:, p_ * 128:(p_ + 1) * 128], ident[:])
                    ApT = aptR.get()
                    mk.cp("act", ApT[:], psZ[:])
                    Psb = psbR.get()
                    St = Sta[d_]
                    Str = Sta_r[d_]
                    Uv = RHall[:].bitcast(F32).rearrange("p (pp s) c -> p pp s c", s=2)
                    Pv = Psb[:].rearrange("p (pp s k) -> p pp s k", s=2, k=64)
                    Yv = yt[:].rearrange("p (pp s k) -> p pp s k", s=2, k=64)
                    dv = dcs[:].rearrange("p (pp c) -> p pp c", c=2)
                    for c_ in corder:
                        r_lo, r_hi = c_ * 64, (c_ + 1) * 64
                        psP = [psA.get(), psA.get()]
                        for s_ in range(2):
                            b0 = s_ * 64
                            for p_ in range(4):
                                stb = Str[b0:b0 + 64, p_ * 128 + b0:p_ * 128 + b0 + 64]
                                mk.mm(psP[s_][:, p_ * 128:p_ * 128 + 64], ApT[b0:b0 + 64, p_ * 128:(p_ + 1) * 128], stb)
                                if want_o:
                                    mk.mm(psP[s_][:, p_ * 128 + 64:(p_ + 1) * 128], FTs[p_][b0:b0 + 64, 128:256], stb)
                        for s_ in range(2):
                            pv = psP[s_][:].rearrange("p (pp c) -> p pp c", c=128)
                            mk.tt("dve", Pv[r_lo:r_hi, :, s_, :], pv[r_lo:r_hi, :, 0:64], Uv[r_lo:r_hi, :, s_, 64:128], ALU.add,
                                  km=allk(RHall))
                            if want_o:
                                mk.cp("act", Yv[r_lo:r_hi, :, s_, :], pv[r_lo:r_hi, :, 64:128])
                        psS = psA.get()
                        for p_ in range(4):
                            pc = slice(p_ * 128, (p_ + 1) * 128)
                            mk.mm(psS[:, pc], Bh[r_lo:r_hi, pc], Psb[r_lo:r_hi, pc], start=True, stop=False)
                            mk.mm(psS[:, pc], Kh[r_lo:r_hi, pc], v_r[r_lo:r_hi, pc], start=False, stop=True)
                        mk.tt("dve", St[:].rearrange("p (pp c) -> p pp c", c=128), St[:].rearrange("p (pp c) -> p pp c", c=128),
                              dv[:, :, c_:c_ + 1].broadcast_to([128, 4, 128]), ALU.mult)
                        mk.tt("dve", St[:], psS[:], St[:], ALU.add)
                        mk.cp("act", Str[:], St[:])
                    if want_o:
                        psQ = psA.get()
                        for h in range(8):
                            hc = slice(h * 64, (h + 1) * 64)
                            mk.mm(psQ[:, hc], AXs[h][:, 128:256], Psb[:, hc], start=True, stop=False)
                            mk.mm(psQ[:, hc], AXs[h][:, 384:512], v_r[:, hc], start=False, stop=True)
                        mk.tt("dve", yt[:], psQ[:], yt[:], ALU.add)
                    if not want_o:
                        continue
                    if d_ == 0:
                        mk.dma("pool", YF[r0_:r1_, :], yt[:], km={"YF_scr": [("YF", ti)]})
                    else:
                        yf = r5.get()
                        mk.dma("sp", yf[:], YF[r0_:r1_, :], km={"YF_scr": [("YF", ti)]})
                        kf = r5.get()
                        mk.dma("sp", kf[:], KF[r0_:r1_, :], km={"KF_scr": [("KF", ti)]})
                        gg = r5.get()
                        mk.dma("sp", gg[:], RG[r0_:r1_, :], km={"RG_scr": [("RG", ti)]})
                        mk.tt("dve", yt[:], yt[:], yf[:], ALU.add)
                        y3 = yt[:].rearrange("p (h k) -> p h k", h=8)
                        mu8 = s8r.get()
                        mk.op("dve", lambda e, mu8=mu8, y3=y3: e.reduce_sum(out=mu8[:], in_=y3, axis=AX.X), [mu8[:]], [yt[:]])
                        mk.ts("dve", mu8[:], mu8[:], 1.0 / 64)
                        mk.tt("dve", y3, y3, mu8[:, :].unsqueeze(2).broadcast_to([128, 8, 64]), ALU.subtract)
                        mk.tt("pool", yf[:], yt[:], yt[:], ALU.mult)
                        v8 = s8r.get()
                        mk.op("dve", lambda e, v8=v8, yf=yf: e.reduce_sum(out=v8[:], in_=yf[:].rearrange("p (h k) -> p h k", h=8), axis=AX.X), [v8[:]], [yf[:]])
                        mk.act(v8[:], v8[:], AF.Ln, bias=64e-5, scale=1.0 / 64)
                        mk.act(v8[:], v8[:], AF.Exp, scale=-0.5)
                        mk.tt("dve", y3, y3, v8[:, :].unsqueeze(2).broadcast_to([128, 8, 64]), ALU.mult)
                        mk.tt("dve", yt[:], yt[:], bcs["gw"][:], ALU.mult)
                        mk.tt("dve", yt[:], yt[:], bcs["gb"][:], ALU.add)
                        mk.tt("dve", kf[:], kf[:], km_[:], ALU.add)
                        mk.tt("dve", kf[:], kf[:], r_, ALU.mult)
                        mk.tt("dve", kf[:], kf[:], bcs["rk"][:], ALU.mult)
                        b8 = s8r.get()
                        mk.op("dve", lambda e, b8=b8, kf=kf: e.reduce_sum(out=b8[:], in_=kf[:].rearrange("p (h k) -> p h k", h=8), axis=AX.X), [b8[:]], [kf[:]])
                        mk.tt("dve", kf[:].rearrange("p (h k) -> p h k", h=8), v_.rearrange("p (h k) -> p h k", h=8),
                              b8[:, :].unsqueeze(2).broadcast_to([128, 8, 64]), ALU.mult)
                        mk.tt("dve", yt[:], yt[:], kf[:], ALU.add)
                        mk.tt("dve", yt[:], yt[:], gg[:], ALU.mult)
                        mk.dma("pool", YC[r0_:r1_, :], yt[:], km={"YC_scr": [("YC_scr", ti)]})
            mk.pop()
            if dbg == "RWKV%d" % l:
                break
            mk.push()
            xsrc = (lambda ti: xz[ti * 128:(ti + 1) * 128, :]) if l == 0 else (lambda ti: XS[ti * 128:(ti + 1) * 128, :])
            gab0 = [mk.sb("gab0%d" % w, [128, D]) for w in range(2)]
            for w in range(2):
                mk.dma("sp", gab0[w][:], MODROW[w:w + 1, 16 * 128:16 * 128 + D].broadcast_to([128, D]))
            stg = Ring(mk, "stg", 2, [128, 2, 1024], F32)
            Wb = mk.sb("Wb", [128, 12, 1024], BF16)
            Wo = mk.sb("Wo", [128, 8, 1024], BF16)
            wbv = w_branch[l].rearrange("i (k p) n -> p (i k) n", p=128)
            wov = w_out[l].rearrange("(k p) n -> p k n", p=128)
            for c_ in range(6):
                sg_ = stg.get()
                mk.dma("sp", sg_[:], wbv[:, 2 * c_:2 * c_ + 2, :])
                mk.cp("pool", Wb[:, 2 * c_:2 * c_ + 2, :], sg_[:])
            for c_ in range(4):
                sg_ = stg.get()
                mk.dma("sp", sg_[:], wov[:, 2 * c_:2 * c_ + 2, :])
                mk.cp("pool", Wo[:, 2 * c_:2 * c_ + 2, :], sg_[:])
            w13s = Ring(mk, "w13s", 2, [128, 2, 8, 128], F32)
            w13c = Ring(mk, "w13c", 2, [128, 2, 8, 128], BF16)
            w1v = w_ffn1[l].rearrange("(k p) n -> p k n", p=128)
            w3v = w_ffn3[l].rearrange("(k p) n -> p k n", p=128)
            def conv_gen():
                for fc in range(22):
                    ws = w13s.get()
                    mk.dma("sp", ws[:, 0, :, :], w1v[:, :, fc * 128:(fc + 1) * 128])
                    mk.dma("sp", ws[:, 1, :, :], w3v[:, :, fc * 128:(fc + 1) * 128])
                    wc = w13c.get()
                    mk.cp("dve", wc[:, 0, :, :], ws[:, 0, :, :])
                    mk.cp("act", wc[:, 1, :, :], ws[:, 1, :, :])
                    mk.dma("pool", W13B[fc], wc[:].rearrange("p a k f -> p (a k f)"), km={"W13B_scr": [("W13B", fc)]})
                    yield
            cgen = conv_gen()
            yin = Ring(mk, "yin", 2, [128, 1536], F32)
            gin_ = Ring(mk, "gin_", 2, [128, 3072], F32)
            yTr = Ring(mk, "yTr", 2, [128, 12, 128], BF16)
            mTr = Ring(mk, "mTr", 2, [128, 8, 128], BF16)
            mS = Ring(mk, "mS", 2, [128, 1024], F32)
            xin2 = Ring(mk, "xin2", 2, [128, 1024], F32)
            ev2 = Ring(mk, "ev2", 4, [128, 512], F32)
            for ti in range(NT):
                if ti < NZ and not need_ctx:
                    continue
                w = 1 if ti < NZ else 0
                r0_, r1_ = ti * 128, (ti + 1) * 128
                next(cgen, None)
                yi = yin.get()
                for i_, Y_ in enumerate((YA, YB, YC)):
                    mk.dma("sp", yi[:, i_ * 512:(i_ + 1) * 512], Y_[r0_:r1_, :], km={Y_.name: [(Y_.name, ti)]})
                gi = gin_.get()
                mk.dma("sp", gi[:], P[r0_:r1_, C_GATE:C_GATE + 3072], km={"P_scr": [("P", ti)]})
                mk.act(gi[:], gi[:], AF.Sigmoid)
                yT = yTr.get()
                for g3 in range(3):
                    psT = psA.get()
                    for c_ in range(4):
                        mk.tr(psT[:, c_ * 128:(c_ + 1) * 128], yi[:, g3 * 512 + c_ * 128:g3 * 512 + (c_ + 1) * 128], ident[:])
                    mk.cp("act", yT[:, g3 * 4:(g3 + 1) * 4, :].rearrange("p c t -> p (c t)"), psT[:])
                m_ = mS.get()
                for i_ in range(3):
                    for half in range(2):
                        ps = psA.get()
                        for k in range(4):
                            mk.mm(ps[:], yT[:, i_ * 4 + k, :], Wb[:, i_ * 4 + k, half * 512:(half + 1) * 512],
                                  start=(k == 0), stop=(k == 3))
                        gsl = gi[:, i_ * 1024 + half * 512:i_ * 1024 + (half + 1) * 512]
                        if i_ == 0:
                            mk.tt("dve", m_[:, half * 512:(half + 1) * 512], ps[:], gsl, ALU.mult)
                        else:
                            e_ = ev2.get()
                            mk.tt("dve", e_[:], ps[:], gsl, ALU.mult)
                            mk.tt("pool", m_[:, half * 512:(half + 1) * 512], m_[:, half * 512:(half + 1) * 512], e_[:], ALU.add)
                mT = mTr.get()
                for half in range(2):
                    psT = psA.get()
                    for c_ in range(4):
                        mk.tr(psT[:, c_ * 128:(c_ + 1) * 128], m_[:, (half * 4 + c_) * 128:(half * 4 + c_ + 1) * 128], ident[:])
                    mk.cp("act", mT[:, half * 4:(half + 1) * 4, :].rearrange("p c t -> p (c t)"), psT[:])
                xt = xin2.get()
                mk.dma("sp", xt[:], xsrc(ti), km={xsrc(ti).name: [(xsrc(ti).name, ti)]})
                for half in range(2):
                    ps = psA.get()
                    for k in range(8):
                        mk.mm(ps[:], mT[:, k, :], Wo[:, k, half * 512:(half + 1) * 512], start=(k == 0), stop=(k == 7))
                    e_ = ev2.get()
                    mk.tt("dve", e_[:], ps[:], gab0[w][:, half * 512:(half + 1) * 512], ALU.mult)
                    mk.tt("pool", xt[:, half * 512:(half + 1) * 512], xt[:, half * 512:(half + 1) * 512], e_[:], ALU.add)
                mk.dma("pool", XS[r0_:r1_, :], xt[:], km={"X_scr": [("X_scr", ti)]})
            for _ in cgen:
                pass
            mk.pop()
            if dbg == "MRG%d" % l:
                break
            mk.push()
            last = (l == L - 1)
            W2 = mk.sb("W2", [128, 22, 1024], BF16)
            stg = Ring(mk, "stg", 2, [128, 2, 1024], F32)
            w2v = w_ffn2[l].rearrange("(k p) n -> p k n", p=128)
            for c_ in range(11):
                sg_ = stg.get()
                mk.dma("sp", sg_[:], w2v[:, 2 * c_:2 * c_ + 2, :])
                mk.cp("pool", W2[:, 2 * c_:2 * c_ + 2, :], sg_[:])
            gab1 = [mk.sb("gab1%d" % w, [128, D]) for w in range(2)]
            for w in range(2):
                mk.dma("sp", gab1[w][:], MODROW[w:w + 1, 40 * 128:40 * 128 + D].broadcast_to([128, D]))
            gfb = mk.sb("gfb", [128, 1024])
            mk.dma("sp", gfb[:], g_final[0:1, :].broadcast_to([128, 1024]))
            xin = Ring(mk, "xin", 5, [128, D], F32)
            sq = mk.sb("sqjunk", [128, D])
            st1 = Ring(mk, "st1", 4, [128, 1], F32)
            dg = Ring(mk, "dg", 2, [128, 128], F32)
            hTb = mk.sb("hTb", [128, 8, 512], BF16)
            actT = mk.sb("actT", [128, 22, 512], BF16)
            w13b = Ring(mk, "w13b", 3, [128, 2, 8, 128], BF16)
            u1s = Ring(mk, "u1s", 2, [128, 512], F32)
            ev2 = Ring(mk, "ev2", 4, [128, 512], F32)
            w1v = w_ffn1[l].rearrange("(k p) n -> p k n", p=128)
            w3v = w_ffn3[l].rearrange("(k p) n -> p k n", p=128)
            tiles = [t_ for t_ in range(NT) if (t_ >= NZ or need_ctx)]
            blocks = [tiles[i_:i_ + 4] for i_ in range(0, len(tiles), 4)]
            for blk in blocks:
                nb = len(blk)
                wdt = nb * 128
                xts = []
                for bi, ti in enumerate(blk):
                    w = 1 if ti < NZ else 0
                    xt = xin.get()
                    xts.append(xt)
                    mk.dma("sp", xt[:], XS[ti * 128:(ti + 1) * 128, :], km={"X_scr": [("X_scr", ti)]})
                    ss = st1.get()
                    mk.act(sq[:], xt[:], AF.Square, accum=ss[:])
                    mk.act(ss[:], ss[:], AF.Ln, bias=1e-6, scale=1.0 / D)
                    mk.act(ss[:], ss[:], AF.Exp, scale=-0.5)
                    dgt = dg.get()
                    mk.ts("dve", dgt[:], ident[:], ss[:, 0:1])
                    for half in range(2):
                        ps = psA.get()
                        for kk in range(4):
                            k = half * 4 + kk
                            mk.mm(ps[:, kk * 128:(kk + 1) * 128], xt[:, k * 128:(k + 1) * 128], dgt[:])
                        for kk in range(4):
                            k = half * 4 + kk
                            mk.act(hTb[:, k, bi * 128:(bi + 1) * 128], ps[:, kk * 128:(kk + 1) * 128], AF.Identity,
                                   bias=modsb[:, 24 + k, w:w + 1], scale=A2[:, k, w:w + 1])
                for fc in range(22):
                    wb_ = w13b.get()
                    mk.dma("sp", wb_[:].rearrange("p a k f -> p (a k f)"), W13B[fc], km={"W13B_scr": [("W13B", fc)]})
                    ps1 = psA.get()
                    for k in range(8):
                        mk.mm(ps1[:, 0:wdt], wb_[:, 0, k, :], hTb[:, k, 0:wdt], start=(k == 0), stop=(k == 7))
                    ps3 = psA.get()
                    for k in range(8):
                        mk.mm(ps3[:, 0:wdt], wb_[:, 1, k, :], hTb[:, k, 0:wdt], start=(k == 0), stop=(k == 7))
                    u1 = u1s.get()
                    mk.act(u1[:, 0:wdt], ps1[:, 0:wdt], AF.Silu)
                    mk.tt("dve", actT[:, fc, 0:wdt], ps3[:, 0:wdt], u1[:, 0:wdt], ALU.mult)
                for bi, ti in enumerate(blk):
                    w = 1 if ti < NZ else 0
                    xt = xts[bi]
                    for half in range(2):
                        ps = psA.get()
                        for k in range(22):
                            mk.mm(ps[:], actT[:, k, bi * 128:(bi + 1) * 128], W2[:, k, half * 512:(half + 1) * 512],
                                  start=(k == 0), stop=(k == 21))
                        e_ = ev2.get()
                        mk.tt("dve", e_[:], ps[:], gab1[w][:, half * 512:(half + 1) * 512], ALU.mult)
                        mk.tt("pool", xt[:, half * 512:(half + 1) * 512], xt[:, half * 512:(half + 1) * 512], e_[:], ALU.add)
                    if not last:
                        mk.dma("pool", XS[ti * 128:(ti + 1) * 128, :], xt[:], km={"X_scr": [("X_scr", ti)]})
                    else:
                        ss = st1.get()
                        mk.act(sq[:], xt[:], AF.Square, accum=ss[:])
                        mk.act(ss[:], ss[:], AF.Ln, bias=1e-6, scale=1.0 / D)
                        mk.act(ss[:], ss[:], AF.Exp, scale=-0.5)
                        mk.stt("dve", xt[:], xt[:], ss[:, 0:1], gfb[:], ALU.mult, ALU.mult)
                        tx = (ti - NZ) * 128
                        mk.dma("pool", yout[tx:tx + 128, :], xt[:], km={"yout": [("yout", ti)]})
                        if dbg is not None:
                            mk.dma("pool", XS[ti * 128:(ti + 1) * 128, :], xt[:], km={"X_scr": [("X_scr", ti)]})
            mk.pop()
            if dbg == "FFN%d" % l:
                break
        if dbg is not None and not NODUMP:
            S.barrier()
            for ti in range(NT):
                if dbg.startswith("P"):
                    for c0 in range(0, INW, 1024):
                        c1 = min(INW, c0 + 1024)
                        mk.dma("sp", dbg_out[ti * 128:(ti + 1) * 128, c0:c1], P[ti * 128:(ti + 1) * 128, c0:c1],
                               km={"P_scr": [("P", ti)]})
                else:
                    for i, Y in enumerate((YA, YB, YC)):
                        mk.dma("sp", dbg_out[ti * 128:(ti + 1) * 128, i * 512:(i + 1) * 512],
                               Y[ti * 128:(ti + 1) * 128, :], km={Y.name: [(Y.name, ti)]})
                    mk.dma("sp", dbg_out[ti * 128:(ti + 1) * 128, 2048:3072],
                           XS[ti * 128:(ti + 1) * 128, :], km={"X_scr": [("X_scr", ti)]})
        S.emit()
    return nc


def host_inputs(inputs, b, NZ=2, NX=32, GRID_W=64):
    f = lambda a: np.ascontiguousarray(np.asarray(a, dtype=np.float32))
    m = {}
    m["xz"] = f(np.concatenate([inputs["ctx"][b], inputs["x"][b]], axis=0))
    c2 = np.stack([np.asarray(inputs["c"][b]), np.asarray(inputs["c_ctx"])], axis=-1)
    m["cT"] = f(c2.reshape(8, 128, 2).transpose(1, 0, 2))
    L = inputs["w_ada"].shape[0]
    m["b_adaT"] = f(np.asarray(inputs["b_ada"]).reshape(L, 48, 128).transpose(0, 2, 1))
    m["g_mixT"] = f(np.asarray(inputs["g_mix"]).reshape(L, 8, 128).transpose(0, 2, 1))
    m["g_ffnT"] = f(np.asarray(inputs["g_ffn"]).reshape(L, 8, 128).transpose(0, 2, 1))
    for k in ("w_ada", "w_in", "w_gk2", "b_gk", "g_gla", "sink", "mu_shift", "w0", "w_w2", "a0", "w_a2", "w_g2",
              "k_k", "k_a", "gn_w", "gn_b", "w_branch", "w_out", "w_ffn1", "w_ffn3", "w_ffn2"):
        m[k] = f(inputs[k])
    m["r_k"] = f(np.asarray(inputs["r_k"]).reshape(L, 512))
    m["g_final"] = f(np.asarray(inputs["g_final"]).reshape(1, D))
    T = NX * 128
    t = np.arange(T)
    row = (t // GRID_W).astype(np.float32)
    col = (t % GRID_W).astype(np.float32)
    inv = (np.float32(10000.0) ** (-np.arange(16, dtype=np.float32) / np.float32(16))).astype(np.float32)
    ar = (row[:, None] * inv[None, :]).astype(np.float32)
    ac = (col[:, None] * inv[None, :]).astype(np.float32)
    cosT = np.concatenate([np.cos(ar), np.cos(ar), np.cos(ac), np.cos(ac)], axis=1)
    sinT = np.concatenate([-np.sin(ar), np.sin(ar), -np.sin(ac), np.sin(ac)], axis=1)
    m["ropeC"] = f(cosT)
    m["ropeS"] = f(sinT)
    return m


_NC_CACHE = {}


def kernel(**inputs):
    B = inputs["x"].shape[0]
    if "nc" not in _NC_CACHE:
        _NC_CACHE["nc"] = build()
    nc = _NC_CACHE["nc"]
    maps = [host_inputs(inputs, b % B) for b in range(8)]
    res = run_bass_kernel_spmd(nc, maps, core_ids=list(range(8)))
    out = np.stack([res.results[b]["yout"] for b in range(B)], axis=0)
    return out.astype(np.float32)
```
